# Optimizing a Trainium2 kernel written in Bass

```python
import jax, jax.numpy as jnp
from jax import lax
import numpy as np

D_MODEL = 2048
BATCH = 2
SEQ = 8192
DEPTH = 1

CHUNK = 64
D_MIX = D_MODEL
CONV_WIDTH = D_MIX // 2
CONV_HEADS = 8
CONV_HEAD_DIM = CONV_WIDTH // CONV_HEADS
CONV_K = 3
POOL_WIDTH = D_MIX - CONV_WIDTH
POOL_WINDOWS = (2, 4, 8, 16)
N_POOL_GROUPS = len(POOL_WINDOWS)
POOL_GROUP_DIM = POOL_WIDTH // N_POOL_GROUPS
IN_PROJ_WIDTH = 3 * CONV_WIDTH + POOL_WIDTH
D_FF = ((8 * D_MODEL // 3 + 255) // 256) * 256
EPS = 1e-6

kernel_name = "hybrid_shortconv_multiscale_pool_block"


def rms_norm(x, g):
    xf = x.astype(jnp.float32)
    y = xf * lax.rsqrt(jnp.mean(xf * xf, axis=-1, keepdims=True) + EPS)
    return (y * g.astype(jnp.float32)).astype(x.dtype)


def rms_norm_plain(x):
    xf = x.astype(jnp.float32)
    y = xf * lax.rsqrt(jnp.mean(xf * xf, axis=-1, keepdims=True) + EPS)
    return y.astype(x.dtype)


def short_conv_causal(u, w):
    c = u.shape[-1]
    rhs = w[:, None, :].astype(u.dtype)
    return lax.conv_general_dilated(
        u, rhs, window_strides=(1,), padding=[(CONV_K - 1, 0)],
        dimension_numbers=("NWC", "WIO", "NWC"), feature_group_count=c)


def multiscale_pool_causal(v):
    bn, s, _ = v.shape
    vg = v.reshape(bn, s, N_POOL_GROUPS, POOL_GROUP_DIM).astype(jnp.float32)
    cs = jnp.cumsum(vg, axis=1)
    pos = jnp.arange(1, s + 1, dtype=jnp.float32)
    outs = []
    for gi, w in enumerate(POOL_WINDOWS):
        c = cs[:, :, gi]
        prev = jnp.pad(c, ((0, 0), (w, 0), (0, 0)))[:, :s]
        cnt = jnp.minimum(pos, float(w))[None, :, None]
        outs.append((c - prev) / cnt - vg[:, :, gi])
    return jnp.stack(outs, axis=2)


def setup_inputs(seed: int = 0) -> dict:
    key = jax.random.key(seed)
    ks = jax.random.split(key, 16)
    L = DEPTH

    def nrm(k, shape, fan_in):
        return jax.random.normal(k, shape, jnp.float32) * (fan_in ** -0.5)

    def gain(k, shape):
        return 1.0 + 0.05 * jax.random.normal(k, shape, jnp.float32)

    return {
        "x": jax.random.normal(ks[0], (BATCH, SEQ, D_MODEL), jnp.float32),
        "ln_mix_pre": gain(ks[1], (L, D_MODEL)),
        "w_in": nrm(ks[2], (L, D_MODEL, IN_PROJ_WIDTH), D_MODEL),
        "conv_w": nrm(ks[3], (L, CONV_K, CONV_WIDTH), CONV_K),
        "pool_w": nrm(ks[4], (L, N_POOL_GROUPS, POOL_GROUP_DIM, POOL_GROUP_DIM), POOL_GROUP_DIM),
        "pool_scale": gain(ks[5], (L, POOL_WIDTH)),
        "w_out": nrm(ks[6], (L, D_MIX, D_MODEL), D_MIX),
        "ln_mix_post": gain(ks[7], (L, D_MODEL)),
        "ln_ffn_pre": gain(ks[8], (L, D_MODEL)),
        "w_gate": nrm(ks[9], (L, D_MODEL, D_FF), D_MODEL),
        "w_up": nrm(ks[10], (L, D_MODEL, D_FF), D_MODEL),
        "w_down": nrm(ks[11], (L, D_FF, D_MODEL), D_FF),
        "ln_ffn_post": gain(ks[12], (L, D_MODEL)),
    }


def reference(x, ln_mix_pre, w_in, conv_w, pool_w, pool_scale, w_out, ln_mix_post,
              ln_ffn_pre, w_gate, w_up, w_down, ln_ffn_post):
    bn, s, _ = x.shape
    for l in range(DEPTH):
        h = rms_norm(x, ln_mix_pre[l])
        proj = jnp.einsum("bsd,de->bse", h, w_in[l])
        gate_b, gate_c, u, v = jnp.split(
            proj, [CONV_WIDTH, 2 * CONV_WIDTH, 3 * CONV_WIDTH], axis=-1)

        y_conv = gate_b * short_conv_causal(gate_c * u, conv_w[l])
        y_conv = rms_norm_plain(y_conv.reshape(bn, s, CONV_HEADS, CONV_HEAD_DIM))
        y_conv = y_conv.reshape(bn, s, CONV_WIDTH)

        pooled = multiscale_pool_causal(v).astype(v.dtype)
        y_pool = jnp.einsum("bsgc,gcd->bsgd", pooled, pool_w[l])
        y_pool = rms_norm_plain(y_pool).reshape(bn, s, POOL_WIDTH) * pool_scale[l]

        mixed = jnp.concatenate([y_conv, y_pool], axis=-1)
        mix_out = jnp.einsum("bse,ed->bsd", mixed, w_out[l])
        x = x + rms_norm(mix_out, ln_mix_post[l])

        hf = rms_norm(x, ln_ffn_pre[l])
        g = jnp.einsum("bsd,df->bsf", hf, w_gate[l])
        up = jnp.einsum("bsd,df->bsf", hf, w_up[l])
        ff = jnp.einsum("bsf,fd->bsd", jax.nn.silu(g) * up, w_down[l])
        x = x + rms_norm(ff, ln_ffn_post[l])
    return x
```

```python
import numpy as np
import concourse.bass as bass
import concourse.mybir as mybir
from concourse.bass_utils import run_bass_kernel_spmd

F32 = mybir.dt.float32
BF16 = mybir.dt.bfloat16
AF = mybir.ActivationFunctionType
ALU = mybir.AluOpType

N_CORES = 8
D = 2048
KC = 16
TT = 512
NTILE = 4
TOK = 2048
HALO = 16
DFF = 5632
FC = 44
EPS = 1e-6
NCONST = 160
C_G0, C_G1, C_G2, C_G3, C_CONV, C_PSC, C_INV = 0, 16, 32, 48, 64, 88, 96
NSC = 4
NSQ = 4
SAME_ENGINE_SYNC = True

ENGS = ("pe", "act", "dve", "pool", "sp")


class R:
    __slots__ = ("kind", "i", "gen")

    def __init__(self, kind, i, gen):
        self.kind, self.i, self.gen = kind, i, gen


class Prog:
    def __init__(self):
        self.gen = {}
        self.ops = {e: [] for e in ENGS}
        self.cnt = {e: 0 for e in ENGS}
        self.res = {}
        self.waited = {e: {} for e in ENGS}
        self.dma_cnt = {}

    def alloc(self, kind, i):
        g = self.gen.get((kind, i), 0) + 1
        self.gen[(kind, i)] = g
        return R(kind, i, g)

    def _norm(self, lst):
        out = []
        for r in lst:
            if isinstance(r, R):
                assert self.gen[(r.kind, r.i)] == r.gen, ("stale handle", r.kind, r.i, r.gen, self.gen[(r.kind, r.i)])
                out.append((r.kind, r.i))
            else:
                out.append(r)
        return out

    def _collect(self, eng, reads, writes):
        need = {}

        def add(tok):
            if tok is None:
                return
            k, v = tok
            if k == eng and (eng == "pe" or not SAME_ENGINE_SYNC):
                return
            if need.get(k, 0) < v:
                need[k] = v

        for r in reads:
            st = self.res.get(r)
            if st is not None:
                add(st["w"])
        for w in writes:
            st = self.res.get(w)
            if st is not None:
                add(st["w"])
                for tk in st["r"]:
                    add(tk)
        waits = []
        wd = self.waited[eng]
        for k, v in need.items():
            if wd.get(k, 0) >= v:
                continue
            wd[k] = v
            waits.append((k, v))
        return waits

    def _update(self, tok, reads, writes):
        for r in reads:
            st = self.res.setdefault(r, {"w": None, "r": []})
            st["r"].append(tok)
        for w in writes:
            self.res[w] = {"w": tok, "r": []}

    def op(self, eng, fn, reads=(), writes=()):
        reads, writes = self._norm(reads), self._norm(writes)
        waits = self._collect(eng, reads, writes)
        self.cnt[eng] += 1
        tok = (eng, self.cnt[eng])
        self.ops[eng].append((waits, fn, True))
        self._update(tok, reads, writes)
        return tok

    def dma(self, eng, fn, semkey, reads=(), writes=(), n=1):
        reads, writes = self._norm(reads), self._norm(writes)
        waits = self._collect(eng, reads, writes)
        self.dma_cnt[semkey] = self.dma_cnt.get(semkey, 0) + 16 * n
        tok = (semkey, self.dma_cnt[semkey])
        self.ops[eng].append((waits, fn, False))
        self._update(tok, reads, writes)
        return tok

    def wait_all(self, eng, toks):
        waits = []
        for k, v in toks:
            if self.waited[eng].get(k, 0) < v:
                self.waited[eng][k] = v
                waits.append((k, v))
        self.ops[eng].append((waits, None, False))

    def replay(self, eng, e, sems):
        for waits, fn, signal in self.ops[eng]:
            for k, v in waits:
                e.wait_ge(sems[k], v)
            if fn is None:
                continue
            inst = fn(e)
            if signal:
                inst.then_inc(sems[eng], 1)


NDBG = 16


def build_program(debug=False):
    nc = bass.Bass("TRN2", target_bir_lowering=False)
    dbg_d = nc.dram_tensor("dbg", [NDBG, 128, TT], F32, kind="ExternalOutput").ap() if debug else None
    xh = nc.dram_tensor("xh", [D, HALO + TOK], F32, kind="ExternalInput").ap()
    consts_d = nc.dram_tensor("consts", [128, NCONST], F32, kind="ExternalInput").ap()
    w_in_d = nc.dram_tensor("wst_in", [8, 128, 8192], F32, kind="ExternalInput").ap()
    pool_w_d = nc.dram_tensor("pool_w", [4, 256, 256], F32, kind="ExternalInput").ap()
    w_out_d = nc.dram_tensor("wst_out", [4, 128, 8192], F32, kind="ExternalInput").ap()
    w_gu_d = nc.dram_tensor("wst_gu", [22, 128, 8192], F32, kind="ExternalInput").ap()
    w_dn_d = nc.dram_tensor("wst_dn", [16, 128, 5632], F32, kind="ExternalInput").ap()
    yT = nc.dram_tensor("yT", [D, TOK], F32, kind="ExternalOutput").ap()

    xh_v = xh.rearrange("(kc p) t -> p kc t", p=128)
    yT_v = yT.rearrange("(kc p) t -> p kc t", p=128)
    pool_w_v = pool_w_d.rearrange("g (kc p) j -> p g kc j", p=128)

    sem_names = [f"d{i}" for i in range(NDBG if debug else 0)] + list(ENGS) + ["s0", "s1", "s2", "x0", "x1", "x2", "x3", "o0", "o1", "o2", "o3", "misc", "pw", "xhl", "xs0", "xs1"]

    from contextlib import ExitStack
    with ExitStack() as es:
        def sb(name, shape, dt):
            return es.enter_context(nc.sbuf_tensor(name, shape, dt))

        xres = sb("xres", [128, KC, TT], F32)
        regA = sb("regA", [128, 32, TT], BF16)
        regB = sb("regB", [128, FC, TT], BF16)
        slots = [sb(f"slot{i}", [128, 8192], BF16) for i in range(3)]
        xgP = sb("xgP", [128, KC, TT], BF16)
        w3 = [sb(f"w3_{i}", [128, TT], F32) for i in range(4)]
        xst = [sb(f"xst{i}", [128, TT], F32) for i in range(2)]
        sct = [sb(f"sc{i}", [128, TT], F32) for i in range(NSC)]
        sqt = [sb(f"sq{i}", [128, TT], BF16) for i in range(NSQ)]
        sc_rs3 = sb("sc_rs3", [128, TT], F32)
        consts = sb("constsb", [128, NCONST], F32)
        poolw = sb("poolw", [128, 4, 2, 256], BF16)
        ones = sb("ones", [128, 128], BF16)
        carry_cu = sb("carry_cu", [128, 8, 2], F32)
        carry_v = sb("carry_v", [128, 8, 16], F32)
        xhalo = sb("xhalo", [128, KC, HALO], F32)
        xgh = sb("xgh", [128, KC, HALO], BF16)
        sqh = sb("sqh", [128, KC, HALO], BF16)
        small = sb("small", [128, 64], F32)
        banks = [es.enter_context(nc.psum_tensor(f"bank{i}", [128, TT], F32)) for i in range(8)]
        sems = {n: es.enter_context(nc.semaphore(n)) for n in sem_names}
        block = es.enter_context(nc.Block())

        P = Prog()
        rr = {"bank": 0, "sc": 0, "sq": 0}
        held = set()

        def next_bank(hold=False):
            for _ in range(9):
                b = rr["bank"]
                rr["bank"] = (b + 1) % 8
                if b not in held:
                    break
            else:
                raise RuntimeError("no free bank")
            if hold:
                held.add(b)
            return P.alloc("ps", b)

        def release(h):
            held.discard(h.i)

        def next_sc():
            i = rr["sc"]
            rr["sc"] = (i + 1) % NSC
            return P.alloc("sc", i)

        def next_sq():
            i = rr["sq"]
            rr["sq"] = (i + 1) % NSQ
            return P.alloc("sq", i)

        def cst(c):
            return consts[:, c:c + 1]

        def xg(k):
            return regA[:, k, :]

        def mixed(k):
            return regA[:, 16 + k, :]

        def ffv(i):
            return regA[:, 2 * i:2 * i + 2, :].bitcast(F32).rearrange("p a b -> p (a b)")

        def mov(i):
            return regB[:, 2 * i:2 * i + 2, :].bitcast(F32).rearrange("p a b -> p (a b)")

        def W5(i):
            return mov(i)

        def K5(i):
            return [("B", 2 * i), ("B", 2 * i + 1)]

        def W8(i):
            return regB[:, 32 + 3 * i:35 + 3 * i, :].bitcast(F32).rearrange("p a b -> p (a b)")[:, 0:528]

        def K8(i):
            return [("B", 32 + 3 * i + d) for d in range(3)]

        def PL(pj):
            return regB[:, 12 + pj, :]

        def KPL(pj):
            return [("B", 12 + pj)]

        def xgp(k):
            return xgP[:, k, :]

        stages = []
        for t in range(NTILE):
            for j in range(8):
                stages.append(("in", j))
            for j in range(4):
                stages.append(("out", j))
            for j in range(22):
                stages.append(("gu", j))
            for c2 in range(8):
                for h in range(2):
                    stages.append(("dn", c2, h))
        st_state = {"next_load": 0, "next_use": 0}

        def emit_load(n):
            st = stages[n]
            s = n % 3
            slot = slots[s]
            if st[0] == "dn":
                dst = slot[:, 0:5632].rearrange("p (a b) -> p a b", b=1408)
                src = w_dn_d[st[1] * 2 + st[2]].rearrange("p (a b) -> p a b", b=1408)
            else:
                srcd = {"in": w_in_d, "out": w_out_d, "gu": w_gu_d}[st[0]]
                dst = slot[:, 0:8192].rearrange("p (a b) -> p a b", b=2048)
                src = srcd[st[1]].rearrange("p (a b) -> p a b", b=2048)
            sk = f"s{s}"
            extra = []
            if n == 0:
                extra = [("x", k) for k in range(KC)]
            elif n in (1, 2):
                extra = [("slot", n - 1)]
            P.dma("pool", lambda e, dst=dst, src=src, sk=sk: e.dma_start(out=dst, in_=src).then_inc(sems[sk], 16),
                  sk, reads=extra, writes=[("slot", s)])

        def use_stage(kind):
            n = st_state["next_use"]
            assert stages[n][0] == kind, (stages[n], kind)
            st_state["next_use"] = n + 1
            while st_state["next_load"] < min(len(stages), n + 3):
                emit_load(st_state["next_load"])
                st_state["next_load"] += 1
            s = n % 3
            if kind == "dn":
                view = slots[s][:, 0:5632].rearrange("p (k n) -> p k n", n=256)
            else:
                view = slots[s][:, 0:8192].rearrange("p (k n) -> p k n", n=512)
            return s, view

        def mm_group(bank, lhs_fn, rhs_fn, nk, reads, c_lo=0, ncols=TT, start=True, stop=True, korder=None,
                     per_k_reads=None):
            ks = list(range(nk)) if korder is None else list(korder)
            if per_k_reads is not None:
                for n, k in enumerate(ks):
                    P.op("pe", lambda e, n=n, k=k: e.matmul(banks[bank.i][:, c_lo:c_lo + ncols], lhsT=lhs_fn(k), rhs=rhs_fn(k),
                                                           start=(start and n == 0), stop=(stop and n == nk - 1)),
                         reads=list(per_k_reads(k)) + list(reads), writes=[bank])
                return

            def fn(e):
                inst = None
                for n, k in enumerate(ks):
                    inst = e.matmul(banks[bank.i][:, c_lo:c_lo + ncols], lhsT=lhs_fn(k), rhs=rhs_fn(k),
                                    start=(start and n == 0), stop=(stop and n == nk - 1))
                return inst
            P.op("pe", fn, reads=reads, writes=[bank])

        def mm_multi(glist, ks, kreads, common, nk_total, done_before):
            for n, k in enumerate(ks):
                pos = done_before + n
                for (bank, lhs_fn, rhs_fn) in glist:
                    P.op("pe", lambda e, bank=bank, lhs_fn=lhs_fn, rhs_fn=rhs_fn, k=k, pos=pos: e.matmul(
                        banks[bank.i][:], lhsT=lhs_fn(k), rhs=rhs_fn(k), start=(pos == 0), stop=(pos == nk_total - 1)),
                         reads=list(kreads(k)) + list(common), writes=[bank])

        def rsqrt_from_bank(bank, n, ncols=TT, fixed=None):
            l = next_sc()
            P.op("act", lambda e: e.activation(out=sct[l.i][:, 0:ncols], in_=banks[bank.i][:, 0:ncols], func=AF.Ln,
                                               scale=1.0 / n, bias=EPS),
                 reads=[bank], writes=[l])
            if fixed is not None:
                P.op("act", lambda e: e.activation(out=fixed[:, 0:ncols], in_=sct[l.i][:, 0:ncols], func=AF.Exp, scale=-0.5),
                     reads=[l], writes=["rs3"])
                return None
            r = next_sc()
            P.op("act", lambda e: e.activation(out=sct[r.i][:, 0:ncols], in_=sct[l.i][:, 0:ncols], func=AF.Exp, scale=-0.5),
                 reads=[l], writes=[r])
            return r

        def act_copy(out, in_, reads, writes):
            P.op("act", lambda e: e.activation(out=out, in_=in_, func=AF.Copy), reads=reads, writes=writes)

        def act_square(out, in_, reads, writes):
            P.op("act", lambda e: e.activation(out=out, in_=in_, func=AF.Square), reads=reads, writes=writes)

        def stat_mm(bank, q, first, last_):
            P.op("pe", lambda e: e.matmul(banks[bank.i][:], lhsT=ones[:], rhs=sqt[q.i][:], start=first, stop=last_),
                 reads=[q, "ones"], writes=[bank])

        P.dma("sp", lambda e: e.dma_start(out=consts[:], in_=consts_d).then_inc(sems["misc"], 16), "misc",
              writes=["consts"])
        P.dma("pool", lambda e: e.dma_start(out=poolw[:], in_=pool_w_v).then_inc(sems["pw"], 16), "pw",
              writes=["poolw"])
        P.op("dve", lambda e: e.memset(ones[:], 1.0), writes=["ones"])

        def dump(idx, src, reads):
            if not debug:
                return
            P.dma("pool", lambda e: e.dma_start(out=dbg_d[idx], in_=src).then_inc(sems[f"d{idx}"], 16), f"d{idx}",
                  reads=reads)

        def emit_tile(t, prev_tail):
            c0 = HALO + t * TT
            last = (t == NTILE - 1)

            def load_xres():
                for q in range(4):
                    P.dma("sp", lambda e, q=q: e.dma_start(out=xres[:, 4 * q:4 * q + 4, :],
                                                           in_=xh_v[:, 4 * q:4 * q + 4, c0:c0 + TT]).then_inc(sems[f"x{q}"], 16),
                          f"x{q}", writes=[("x", k) for k in range(4 * q, 4 * q + 4)])
            if t == 0:
                load_xres()
                P.dma("sp", lambda e: e.dma_start(out=xhalo[:], in_=xh_v[:, :, 0:HALO]).then_inc(sems["xhl"], 16),
                      "xhl", writes=["xhalo"])
                sb0 = next_bank(hold=True)
                for kc in range(KC):
                    q = next_sq()
                    act_square(sqt[q.i][:], xres[:, kc, :], [("x", kc)], [q])
                    stat_mm(sb0, q, kc == 0, kc == KC - 1)
                r0 = rsqrt_from_bank(sb0, D)
                release(sb0)
                dump(0, sct[r0.i][:], [r0])
                for kc in range(KC):
                    P.op("dve", lambda e, kc=kc: e.scalar_tensor_tensor(out=xgp(kc), in0=xres[:, kc, :], scalar=cst(C_G0 + kc),
                                                                        in1=sct[r0.i][:], op0=ALU.mult, op1=ALU.mult),
                         reads=[("x", kc), r0, "consts"], writes=[("XP", kc)])
                dump(1, xgp(0), [("XP", 0)])
                dump(15, xgp(15), [("XP", 15)])
                act_square(sqh[:], xhalo[:], ["xhalo"], ["sqh"])
                sbh = next_bank()
                mm_group(sbh, lambda k: ones[:], lambda k: sqh[:, k, :], KC, reads=["sqh", "ones"], ncols=HALO)
                rh = rsqrt_from_bank(sbh, D, ncols=HALO)
                for kc in range(KC):
                    P.op("dve", lambda e, kc=kc: e.scalar_tensor_tensor(out=xgh[:, kc, :], in0=xhalo[:, kc, :],
                                                                        scalar=cst(C_G0 + kc), in1=sct[rh.i][:, 0:HALO],
                                                                        op0=ALU.mult, op1=ALU.mult),
                         reads=["xhalo", rh, "consts"], writes=["xgh"])

            def make_prefetch(tn):
                cn = HALO + tn * TT
                st = {"bank": None, "r": None, "n": 0}

                def load(kc):
                    i = st["n"] % 2
                    st["n"] += 1
                    P.dma("sp", lambda e: e.dma_start(out=xst[i][:], in_=xh_v[:, kc, cn:cn + TT]).then_inc(sems[f"xs{i}"], 16),
                          f"xs{i}", writes=[("xst", i)])
                    return i

                steps = []
                pending = {}

                def s_load(n):
                    def f():
                        pending[n] = load(n % KC)
                    return f

                def s_comp(n):
                    def f():
                        i = pending.pop(n)
                        kc = n % KC
                        if n < KC:
                            if n == 0:
                                st["bank"] = next_bank(hold=True)
                            q = next_sq()
                            act_square(sqt[q.i][:], xst[i][:], [("xst", i)], [q])
                            stat_mm(st["bank"], q, kc == 0, kc == KC - 1)
                            if kc == KC - 1:
                                st["r"] = rsqrt_from_bank(st["bank"], D)
                                release(st["bank"])
                        else:
                            r = st["r"]
                            P.op("dve", lambda e: e.scalar_tensor_tensor(out=xgp(kc), in0=xst[i][:], scalar=cst(C_G0 + kc),
                                                                         in1=sct[r.i][:], op0=ALU.mult, op1=ALU.mult),
                                 reads=[("xst", i), r, "consts"], writes=[("XP", kc)])
                    return f

                steps.append(s_load(0))
                for n in range(2 * KC):
                    if n + 1 < 2 * KC:
                        steps.append(s_load(n + 1))
                    steps.append(s_comp(n))
                return steps

            U = [dict() for _ in range(8)]
            xg_reads = [("XP", k) for k in range(KC)]

            def M(j):
                s, wv = use_stage("in")
                u = U[j]
                if t == 0 and j == 0:
                    gl = []
                    for name, col in (("C", 1), ("u", 2), ("v", 3), ("B", 0)):
                        b = next_bank()
                        u[name] = b
                        gl.append((b, (lambda k, col=col: wv[:, k, col * 128:(col + 1) * 128]), (lambda k: xgp(k))))
                    mm_multi(gl, range(KC), lambda k: [("XP", k)], [("slot", s)], KC, 0)
                else:
                    for name, col in (("C", 1), ("u", 2), ("v", 3), ("B", 0)):
                        b = next_bank()
                        u[name] = b
                        mm_group(b, lambda k, col=col: wv[:, k, col * 128:(col + 1) * 128], lambda k: xgp(k), KC,
                                 reads=xg_reads + [("slot", s)])
                if t == 0:
                    b = next_bank()
                    u["halo"] = b
                    for hi, col in enumerate((1, 2, 3)):
                        mm_group(b, lambda k, col=col: wv[:, k, col * 128:(col + 1) * 128], lambda k: xgh[:, k, :], KC,
                                 reads=["xgh", ("slot", s)], c_lo=16 * hi, ncols=HALO)

            def E(j):
                u = U[j]
                HQ = {"h": [], "q": []}
                cur_list = ["h"]

                def OP(eng, fn, reads, writes):
                    HQ[cur_list[0]].append((eng, fn, reads, writes))

                def ACOPY(out, in_, reads, writes):
                    OP("act", lambda e: e.activation(out=out, in_=in_, func=AF.Copy), reads, writes)
                hC, hp, hcu = K5(0), K5(1), K8(0)
                hy = K5(2 + j % 2)
                Csb, p, cu, y = W5(0), W5(1), W8(0), W5(2 + j % 2)
                u["y"] = 2 + j % 2
                bC, bu, bB, bv = u["C"], u["u"], u["B"], u["v"]
                ACOPY(Csb, banks[bC.i][:], [bC], hC)
                OP("dve", lambda e: e.tensor_tensor(out=cu[:, 2:2 + TT], in0=Csb, in1=banks[bu.i][:], op=ALU.mult),
                     reads=hC + [bu], writes=hcu)
                if t == 0:
                    bh = u["halo"]
                    ACOPY(small[:, 0:2], banks[bh.i][:, 14:16], [bh], ["small"])
                    OP("dve", lambda e: e.tensor_tensor(out=cu[:, 0:2], in0=small[:, 0:2], in1=banks[bh.i][:, 30:32],
                                                          op=ALU.mult),
                         reads=["small", bh], writes=hcu)
                else:
                    ACOPY(cu[:, 0:2], carry_cu[:, j, :], [("ccu", j)], hcu)
                if not last:
                    ACOPY(carry_cu[:, j, :], cu[:, TT:TT + 2], hcu, [("ccu", j)])
                OP("dve", lambda e: e.tensor_scalar(out=p, in0=cu[:, 2:2 + TT], scalar1=cst(C_CONV + 2 * 8 + j),
                                                      scalar2=None, op0=ALU.mult),
                     reads=hcu + ["consts"], writes=hp)
                OP("dve", lambda e: e.scalar_tensor_tensor(out=p, in0=cu[:, 1:1 + TT], scalar=cst(C_CONV + 1 * 8 + j),
                                                             in1=p, op0=ALU.mult, op1=ALU.add),
                     reads=hcu + hp + ["consts"], writes=hp)
                OP("dve", lambda e: e.scalar_tensor_tensor(out=p, in0=cu[:, 0:TT], scalar=cst(C_CONV + 0 * 8 + j),
                                                             in1=p, op0=ALU.mult, op1=ALU.add),
                     reads=hcu + hp + ["consts"], writes=hp)
                OP("dve", lambda e: e.tensor_tensor(out=y, in0=p, in1=banks[bB.i][:], op=ALU.mult),
                     reads=hp + [bB], writes=hy)
                q = next_sq()
                u["ysq"] = q
                OP("act", lambda e: e.activation(out=sqt[q.i][:], in_=y, func=AF.Square), hy, [q])
                cur_list[0] = "q"
                hv, hA, hB = K8(1), K8(2), K8(3)
                vb, sA, sB = W8(1), W8(2), W8(3)
                g = j // 2
                wwin = 2 << g
                ACOPY(vb[:, 16:16 + TT], banks[bv.i][:], [bv], hv)
                if t == 0:
                    ACOPY(vb[:, 0:16], banks[bh.i][:, 32:48], [bh], hv)
                else:
                    ACOPY(vb[:, 0:16], carry_v[:, j, :], [("cv", j)], hv)
                if not last:
                    ACOPY(carry_v[:, j, :], vb[:, TT:TT + 16], hv, [("cv", j)])
                OP("dve", lambda e: e.tensor_tensor(out=sA[:, 1:528], in0=vb[:, 1:528], in1=vb[:, 0:527], op=ALU.add),
                     reads=hv, writes=hA)
                cur, hcur = sA, hA
                if g >= 1:
                    OP("dve", lambda e: e.tensor_tensor(out=sB[:, 3:528], in0=sA[:, 3:528], in1=sA[:, 1:526], op=ALU.add),
                         reads=hA, writes=hB)
                    cur, hcur = sB, hB
                if g >= 2:
                    OP("dve", lambda e: e.tensor_tensor(out=sA[:, 7:528], in0=sB[:, 7:528], in1=sB[:, 3:524], op=ALU.add),
                         reads=hB, writes=hA)
                    cur, hcur = sA, hA
                if g >= 3:
                    OP("dve", lambda e: e.tensor_tensor(out=sB[:, 15:528], in0=sA[:, 15:528], in1=sA[:, 7:520], op=ALU.add),
                         reads=hA, writes=hB)
                    cur, hcur = sB, hB
                pj = j % 4
                OP("dve", lambda e: e.scalar_tensor_tensor(out=PL(pj), in0=cur[:, 16:16 + TT], scalar=1.0 / wwin,
                                                             in1=vb[:, 16:16 + TT], op0=ALU.mult, op1=ALU.subtract),
                     reads=hcur + hv, writes=KPL(pj))
                if t == 0:
                    OP("dve", lambda e: e.tensor_tensor(out=small[:, 16:32], in0=cur[:, 16:32],
                                                          in1=consts[:, C_INV + g * 16:C_INV + (g + 1) * 16], op=ALU.mult),
                         reads=hcur + ["consts"], writes=["small2"])
                    OP("dve", lambda e: e.tensor_tensor(out=PL(pj)[:, 0:16], in0=small[:, 16:32], in1=vb[:, 16:32],
                                                          op=ALU.subtract),
                         reads=["small2"] + hv, writes=KPL(pj))
                hl, ql = HQ["h"], HQ["q"]
                for n in range(max(len(hl), len(ql))):
                    if n < len(hl):
                        P.op(hl[n][0], hl[n][1], reads=hl[n][2], writes=hl[n][3])
                    if n < len(ql):
                        P.op(ql[n][0], ql[n][1], reads=ql[n][2], writes=ql[n][3])
                if t == 0 and j == 0:
                    dump(2, cu[:, 2:2 + TT], hcu)
                    dump(14, Csb, hC)
                    dump(3, y, hy)
                    dump(5, PL(pj), KPL(pj))

            def S(j):
                u = U[j]
                b = next_bank()
                u["hst"] = b
                stat_mm(b, u["ysq"], True, True)
                if j % 2 == 1:
                    g = j // 2
                    u["yp"] = []
                    for e2 in range(2):
                        b2 = next_bank()
                        u["yp"].append(b2)
                        mm_group(b2, lambda k, e2=e2: poolw[:, g, k, e2 * 128:(e2 + 1) * 128],
                                 lambda k: PL((2 * g + k) % 4), 2,
                                 reads=KPL((2 * g) % 4) + KPL((2 * g + 1) % 4) + ["poolw"])

            def F(j):
                u = U[j]
                r = rsqrt_from_bank(u["hst"], 128)
                iy = u["y"]
                P.op("dve", lambda e: e.tensor_tensor(out=mixed(j), in0=W5(iy), in1=sct[r.i][:], op=ALU.mult),
                     reads=K5(iy) + [r], writes=[("A", 16 + j)])
                if t == 0 and j == 0:
                    dump(4, mixed(0), [("A", 16)])
                if j % 2 == 1:
                    u["yps"] = []
                    u["ypsq"] = []
                    for e2 in range(2):
                        iw = 4 + e2
                        q = next_sq()
                        b2 = u["yp"][e2]
                        u["yps"].append(iw)
                        u["ypsq"].append(q)
                        act_copy(W5(iw), banks[b2.i][:], [b2], K5(iw))
                        act_square(sqt[q.i][:], banks[b2.i][:], [b2], [q])

            def S2(j):
                if j % 2 != 1:
                    return
                u = U[j]
                b = next_bank()
                u["pst"] = b
                for e2 in range(2):
                    stat_mm(b, u["ypsq"][e2], e2 == 0, e2 == 1)

            def F2(j):
                if j % 2 != 1:
                    return
                u = U[j]
                g = j // 2
                r = rsqrt_from_bank(u["pst"], 256)
                for e2 in range(2):
                    iw = u["yps"][e2]
                    c = 2 * g + e2
                    P.op("dve", lambda e, iw=iw, c=c: e.scalar_tensor_tensor(out=mixed(8 + c), in0=W5(iw),
                                                                             scalar=cst(C_PSC + c), in1=sct[r.i][:],
                                                                             op0=ALU.mult, op1=ALU.mult),
                         reads=K5(iw) + [r, "consts"], writes=[("A", 16 + 8 + c)])
                if t == 0 and j == 1:
                    dump(6, mixed(8), [("A", 24)])

            OUT_EARLY = [0, 1, 2, 3, 4, 5, 6, 8, 9, 10, 11, 12, 13]
            OUT_LATE = [7, 14, 15]
            op0 = {}

            def out_stage0_early():
                s0, wv0 = use_stage("out")
                op0["s"], op0["wv"] = s0, wv0
                op0["banks"] = [next_bank(hold=True) for _ in range(4)]
                op0["gl"] = [(op0["banks"][i], (lambda k, i=i: wv0[:, k, i * 128:(i + 1) * 128]), (lambda k: mixed(k)))
                             for i in range(4)]
                mm_multi(op0["gl"], OUT_EARLY, lambda k: [("A", 16 + k)], [("slot", s0)], KC, 0)

            for step in range(8 + 2):
                if step < 8:
                    M(step)
                if step == 8:
                    out_stage0_early()
                if 0 <= step - 1 < 8:
                    S(step - 1)
                if 0 <= step - 2 < 8:
                    S2(step - 2)
                if step < 8:
                    E(step)
                if step < 8 and prev_tail:
                    for st_ in prev_tail[2 * step:2 * step + 2]:
                        st_()
                    if step == 7:
                        load_xres()
                if 0 <= step - 1 < 8:
                    F(step - 1)
                if 0 <= step - 2 < 8:
                    F2(step - 2)

            mixed_reads = [("A", 16 + k) for k in range(KC)]
            sb1 = next_bank(hold=True)
            pend = None

            def evac_mo(i, b):
                q = next_sq()
                act_square(sqt[q.i][:], banks[b.i][:], [b], [q])
                stat_mm(sb1, q, i == 0, i == KC - 1)
                act_copy(mov(i), banks[b.i][:], [b], [("B", 2 * i), ("B", 2 * i + 1)])

            mm_multi(op0["gl"], OUT_LATE, lambda k: [("A", 16 + k)], [("slot", op0["s"])], KC, len(OUT_EARLY))
            for b in op0["banks"]:
                release(b)
            for i in range(3):
                evac_mo(i, op0["banks"][i])
            pend = (3, op0["banks"][3])
            for i in range(4, KC):
                if i % 4 == 0:
                    s, wv = use_stage("out")
                b = next_bank()
                mm_group(b, lambda k, i=i, wv=wv: wv[:, k, (i % 4) * 128:(i % 4 + 1) * 128], lambda k: mixed(k), KC,
                         reads=mixed_reads + [("slot", s)])
                if pend is not None:
                    evac_mo(*pend)
                pend = (i, b)
            evac_mo(*pend)
            rs1 = rsqrt_from_bank(sb1, D)
            release(sb1)
            sb2 = next_bank(hold=True)
            def x1_stt(i):
                P.op("dve", lambda e, i=i: e.scalar_tensor_tensor(out=mov(i), in0=mov(i), scalar=cst(C_G1 + i),
                                                                  in1=sct[rs1.i][:], op0=ALU.mult, op1=ALU.mult),
                     reads=[("B", 2 * i), ("B", 2 * i + 1), rs1, "consts"], writes=[("B", 2 * i), ("B", 2 * i + 1)])

            x1_stt(0)
            for i in range(KC):
                if i + 1 < KC:
                    x1_stt(i + 1)
                P.op("dve", lambda e, i=i: e.tensor_tensor(out=xres[:, i, :], in0=mov(i), in1=xres[:, i, :], op=ALU.add),
                     reads=[("B", 2 * i), ("B", 2 * i + 1), ("x", i)], writes=[("x", i)])
                q = next_sq()
                act_square(sqt[q.i][:], xres[:, i, :], [("x", i)], [q])
                stat_mm(sb2, q, i == 0, i == KC - 1)
                P.op("act", lambda e, i=i: e.activation(out=xg(i), in_=xres[:, i, :], func=AF.Copy, scale=cst(C_G2 + i)),
                     reads=[("x", i), "consts"], writes=[("A", i)])
            r2 = rsqrt_from_bank(sb2, D)
            release(sb2)
            if t == 0:
                dump(7, xres[:, 0, :], [("x", 0)])
                dump(11, sct[rs1.i][:], [rs1])
                dump(12, sct[r2.i][:], [r2])

            hf_reads = [("A", k) for k in range(KC)]
            pend = None
            sgi = [0]

            pf_steps = make_prefetch(t + 1) if not last else []
            pf_pos = [0]

            def pf_advance(n):
                for _ in range(n):
                    if pf_pos[0] < len(pf_steps):
                        pf_steps[pf_pos[0]]()
                        pf_pos[0] += 1

            def evac_gu(f, bg, bu):
                hg = P.alloc("w3", 2 * (sgi[0] % 2))
                hu = P.alloc("w3", 2 * (sgi[0] % 2) + 1)
                sgi[0] += 1
                tg, tu = w3[hg.i], w3[hu.i]
                P.op("dve", lambda e: e.tensor_tensor(out=tg[:], in0=banks[bg.i][:], in1=sct[r2.i][:], op=ALU.mult),
                     reads=[bg, r2], writes=[hg])
                P.op("dve", lambda e: e.tensor_tensor(out=tu[:], in0=banks[bu.i][:], in1=sct[r2.i][:], op=ALU.mult),
                     reads=[bu, r2], writes=[hu])
                P.op("act", lambda e: e.activation(out=tg[:], in_=tg[:], func=AF.Silu),
                     reads=[hg], writes=[hg])
                P.op("dve", lambda e: e.tensor_tensor(out=regB[:, f, :], in0=tg[:], in1=tu[:], op=ALU.mult),
                     reads=[hg, hu], writes=[("B", f)])

            for f in range(FC):
                if f % 2 == 0:
                    s, wv = use_stage("gu")
                e2 = f % 2
                bg = next_bank()
                mm_group(bg, lambda k, wv=wv, e2=e2: wv[:, k, e2 * 128:(e2 + 1) * 128], lambda k: xg(k), KC,
                         reads=hf_reads + [("slot", s)])
                bu = next_bank()
                mm_group(bu, lambda k, wv=wv, e2=e2: wv[:, k, 256 + e2 * 128:256 + (e2 + 1) * 128], lambda k: xg(k), KC,
                         reads=hf_reads + [("slot", s)])
                if pend is not None:
                    evac_gu(*pend)
                pend = (f, bg, bu)
                pf_advance(2)
            evac_gu(*pend)
            pf_advance(len(pf_steps))
            if t == 0:
                dump(9, regB[:, 0, :], [("B", 0)])

            sb3 = next_bank(hold=True)
            pend = []

            def evac_ff(i, b):
                q = next_sq()
                act_square(sqt[q.i][:], banks[b.i][:], [b], [q])
                stat_mm(sb3, q, i == 0, i == KC - 1)
                act_copy(ffv(i), banks[b.i][:], [b], [("A", 2 * i), ("A", 2 * i + 1)])

            for c2 in range(8):
                bb = [next_bank(), next_bank()]
                for h in range(2):
                    s, wv = use_stage("dn")
                    for e2 in range(2):
                        mm_group(bb[e2], lambda k, wv=wv, e2=e2: wv[:, k, e2 * 128:(e2 + 1) * 128],
                                 lambda k, h=h: regB[:, h * 22 + k, :], 22,
                                 reads=[("B", h * 22 + k) for k in range(22)] + [("slot", s)],
                                 start=(h == 0), stop=(h == 1))
                for (i, b) in pend:
                    evac_ff(i, b)
                pend = [(2 * c2, bb[0]), (2 * c2 + 1, bb[1])]
            for (i, b) in pend:
                evac_ff(i, b)
            rsqrt_from_bank(sb3, D, fixed=sc_rs3)
            release(sb3)
            if t == 0:
                dump(10, ffv(0), [("A", 0), ("A", 1)])
                dump(13, sc_rs3[:], ["rs3"])

            def tail_step(i):
                P.op("dve", lambda e: e.scalar_tensor_tensor(out=ffv(i), in0=ffv(i), scalar=cst(C_G3 + i),
                                                             in1=sc_rs3[:], op0=ALU.mult, op1=ALU.mult),
                     reads=[("A", 2 * i), ("A", 2 * i + 1), "rs3", "consts"], writes=[("A", 2 * i), ("A", 2 * i + 1)])
                P.op("dve", lambda e: e.tensor_tensor(out=xres[:, i, :], in0=ffv(i), in1=xres[:, i, :], op=ALU.add),
                     reads=[("A", 2 * i), ("A", 2 * i + 1), ("x", i)], writes=[("x", i)])
                if i % 4 == 3:
                    q = i // 4
                    P.dma("sp", lambda e: e.dma_start(out=yT_v[:, 4 * q:4 * q + 4, t * TT:(t + 1) * TT],
                                                      in_=xres[:, 4 * q:4 * q + 4, :]).then_inc(sems[f"o{q}"], 16),
                          f"o{q}", reads=[("x", k) for k in range(4 * q, 4 * q + 4)])

            order = [8, 12, 9, 13, 10, 14, 11, 15, 0, 1, 2, 3, 4, 5, 6, 7]
            steps = [(lambda i=i: tail_step(i)) for i in order]
            if last:
                for st_ in steps:
                    st_()
                return []
            return steps

        tail = []
        for t in range(NTILE):
            tail = emit_tile(t, tail)
        P.wait_all("sp", [(f"o{q}", P.dma_cnt[f"o{q}"]) for q in range(4)])

        @block.sync
        def _(e):
            P.replay("sp", e, sems)

        @block.gpsimd
        def _(e):
            P.replay("pool", e, sems)

        @block.tensor
        def _(e):
            P.replay("pe", e, sems)

        @block.scalar
        def _(e):
            P.replay("act", e, sems)

        @block.vector
        def _(e):
            P.replay("dve", e, sems)

    return nc


_PROGRAM = None


def _host_layout(x, ln_mix_pre, w_in, conv_w, pool_w, pool_scale, w_out, ln_mix_post,
                 ln_ffn_pre, w_gate, w_up, w_down, ln_ffn_post):
    f32 = np.float32
    x = np.asarray(x, f32)
    w_in = np.asarray(w_in, f32)[0]
    w_in_p = w_in.reshape(D, 4, 8, 128).transpose(0, 2, 1, 3).reshape(D, 4096)
    wg = np.asarray(w_gate, f32)[0].reshape(D, 22, 256)
    wu = np.asarray(w_up, f32)[0].reshape(D, 22, 256)
    w_gu = np.stack([wg, wu], axis=2).reshape(D, 2 * DFF)

    def stage_major(w, ncol):
        K, N = w.shape
        return np.ascontiguousarray(
            w.reshape(K // 128, 128, N // ncol, ncol).transpose(2, 1, 0, 3).reshape(N // ncol, 128, (K // 128) * ncol))

    wst_in = stage_major(w_in_p, 512)
    wst_out = stage_major(np.asarray(w_out, f32)[0], 512)
    wst_gu = stage_major(w_gu, 512)
    wd = np.asarray(w_down, f32)[0]
    wst_dn = np.ascontiguousarray(
        wd.reshape(2, 22, 128, 8, 256).transpose(3, 0, 2, 1, 4).reshape(16, 128, 22 * 256))
    pool_w_ = np.ascontiguousarray(np.asarray(pool_w, f32)[0])

    def pk(v):
        return np.asarray(v, f32).reshape(-1, 128).T

    base = np.zeros((128, NCONST), f32)
    base[:, C_G0:C_G0 + 16] = pk(ln_mix_pre[0])
    base[:, C_G1:C_G1 + 16] = pk(ln_mix_post[0])
    base[:, C_G2:C_G2 + 16] = pk(ln_ffn_pre[0])
    base[:, C_G3:C_G3 + 16] = pk(ln_ffn_post[0])
    cw = np.asarray(conv_w, f32)[0]
    for j in range(3):
        base[:, C_CONV + j * 8:C_CONV + (j + 1) * 8] = cw[j].reshape(8, 128).T
    base[:, C_PSC:C_PSC + 8] = np.asarray(pool_scale, f32)[0].reshape(8, 128).T

    in_maps = []
    for c in range(N_CORES):
        b, q = divmod(c, 4)
        xh = np.zeros((D, HALO + TOK), f32)
        lo = q * TOK - HALO
        if lo >= 0:
            xh[:, :] = x[b, lo:(q + 1) * TOK, :].T
        else:
            xh[:, HALO:] = x[b, 0:TOK, :].T
        cst = base.copy()
        for g, w in enumerate((2, 4, 8, 16)):
            for i in range(16):
                cnt = min(i + 1, w) if q == 0 else w
                cst[:, C_INV + g * 16 + i] = 1.0 / cnt
        in_maps.append({"xh": xh, "consts": cst, "wst_in": wst_in, "pool_w": pool_w_, "wst_out": wst_out,
                        "wst_gu": wst_gu, "wst_dn": wst_dn})
    return in_maps


def kernel(x, ln_mix_pre, w_in, conv_w, pool_w, pool_scale, w_out, ln_mix_post,
           ln_ffn_pre, w_gate, w_up, w_down, ln_ffn_post):
    global _PROGRAM
    in_maps = _host_layout(x, ln_mix_pre, w_in, conv_w, pool_w, pool_scale, w_out, ln_mix_post,
                           ln_ffn_pre, w_gate, w_up, w_down, ln_ffn_post)
    if _PROGRAM is None:
        _PROGRAM = build_program()
    res = run_bass_kernel_spmd(_PROGRAM, in_maps, core_ids=list(range(N_CORES)))
    out = np.empty((2, 4 * TOK, D), np.float32)
    for c in range(N_CORES):
        b, q = divmod(c, 4)
        out[b, q * TOK:(q + 1) * TOK, :] = np.asarray(res.results[c]["yT"], np.float32).T
    return out
```

```python
import numpy as np
import concourse.bass as bass
import concourse.mybir as mybir
from concourse.bass_utils import run_bass_kernel_spmd

F32 = mybir.dt.float32
BF16 = mybir.dt.bfloat16
AF = mybir.ActivationFunctionType
ALU = mybir.AluOpType

N_CORES = 8
D = 2048
KC = 16
TT = 512
NTILE = 4
TOK = 2048
HALO = 16
DFF = 5632
FC = 44
EPS = 1e-6
NCONST = 160
C_G0, C_G1, C_G2, C_G3, C_CONV, C_PSC, C_INV = 0, 16, 32, 48, 64, 88, 96
NSC = 4
NSQ = 4
SAME_ENGINE_SYNC = True

ENGS = ("pe", "act", "dve", "pool", "sp")


class R:
    __slots__ = ("kind", "i", "gen")

    def __init__(self, kind, i, gen):
        self.kind, self.i, self.gen = kind, i, gen


class Prog:
    def __init__(self):
        self.gen = {}
        self.ops = {e: [] for e in ENGS}
        self.cnt = {e: 0 for e in ENGS}
        self.res = {}
        self.waited = {e: {} for e in ENGS}
        self.dma_cnt = {}

    def alloc(self, kind, i):
        g = self.gen.get((kind, i), 0) + 1
        self.gen[(kind, i)] = g
        return R(kind, i, g)

    def _norm(self, lst):
        out = []
        for r in lst:
            if isinstance(r, R):
                assert self.gen[(r.kind, r.i)] == r.gen, ("stale handle", r.kind, r.i, r.gen, self.gen[(r.kind, r.i)])
                out.append((r.kind, r.i))
            else:
                out.append(r)
        return out

    def _collect(self, eng, reads, writes):
        need = {}

        def add(tok):
            if tok is None:
                return
            k, v = tok
            if k == eng and (eng == "pe" or not SAME_ENGINE_SYNC):
                return
            if need.get(k, 0) < v:
                need[k] = v

        for r in reads:
            st = self.res.get(r)
            if st is not None:
                add(st["w"])
        for w in writes:
            st = self.res.get(w)
            if st is not None:
                add(st["w"])
                for tk in st["r"]:
                    add(tk)
        waits = []
        wd = self.waited[eng]
        for k, v in need.items():
            if wd.get(k, 0) >= v:
                continue
            wd[k] = v
            waits.append((k, v))
        return waits

    def _update(self, tok, reads, writes):
        for r in reads:
            st = self.res.setdefault(r, {"w": None, "r": []})
            st["r"].append(tok)
        for w in writes:
            self.res[w] = {"w": tok, "r": []}

    def op(self, eng, fn, reads=(), writes=()):
        reads, writes = self._norm(reads), self._norm(writes)
        waits = self._collect(eng, reads, writes)
        self.cnt[eng] += 1
        tok = (eng, self.cnt[eng])
        self.ops[eng].append((waits, fn, True))
        self._update(tok, reads, writes)
        return tok

    def dma(self, eng, fn, semkey, reads=(), writes=(), n=1):
        reads, writes = self._norm(reads), self._norm(writes)
        waits = self._collect(eng, reads, writes)
        self.dma_cnt[semkey] = self.dma_cnt.get(semkey, 0) + 16 * n
        tok = (semkey, self.dma_cnt[semkey])
        self.ops[eng].append((waits, fn, False))
        self._update(tok, reads, writes)
        return tok

    def wait_all(self, eng, toks):
        waits = []
        for k, v in toks:
            if self.waited[eng].get(k, 0) < v:
                self.waited[eng][k] = v
                waits.append((k, v))
        self.ops[eng].append((waits, None, False))

    def replay(self, eng, e, sems):
        for waits, fn, signal in self.ops[eng]:
            for k, v in waits:
                e.wait_ge(sems[k], v)
            if fn is None:
                continue
            inst = fn(e)
            if signal:
                inst.then_inc(sems[eng], 1)


NDBG = 16


def build_program(debug=False):
    nc = bass.Bass("TRN2", target_bir_lowering=False)
    dbg_d = nc.dram_tensor("dbg", [NDBG, 128, TT], F32, kind="ExternalOutput").ap() if debug else None
    xh = nc.dram_tensor("xh", [D, HALO + TOK], F32, kind="ExternalInput").ap()
    consts_d = nc.dram_tensor("consts", [128, NCONST], F32, kind="ExternalInput").ap()
    w_in_d = nc.dram_tensor("wst_in", [8, 128, 8192], F32, kind="ExternalInput").ap()
    pool_w_d = nc.dram_tensor("pool_w", [4, 256, 256], F32, kind="ExternalInput").ap()
    w_out_d = nc.dram_tensor("wst_out", [4, 128, 8192], F32, kind="ExternalInput").ap()
    w_gu_d = nc.dram_tensor("wst_gu", [22, 128, 8192], F32, kind="ExternalInput").ap()
    w_dn_d = nc.dram_tensor("wst_dn", [16, 128, 5632], F32, kind="ExternalInput").ap()
    yT = nc.dram_tensor("yT", [D, TOK], F32, kind="ExternalOutput").ap()

    xh_v = xh.rearrange("(kc p) t -> p kc t", p=128)
    yT_v = yT.rearrange("(kc p) t -> p kc t", p=128)
    pool_w_v = pool_w_d.rearrange("g (kc p) j -> p g kc j", p=128)

    sem_names = [f"d{i}" for i in range(NDBG if debug else 0)] + list(ENGS) + ["s0", "s1", "s2", "x0", "x1", "x2", "x3", "o0", "o1", "o2", "o3", "misc", "pw", "xhl", "xs0", "xs1"]

    from contextlib import ExitStack
    with ExitStack() as es:
        def sb(name, shape, dt):
            return es.enter_context(nc.sbuf_tensor(name, shape, dt))

        xres = sb("xres", [128, KC, TT], F32)
        regA = sb("regA", [128, 32, TT], BF16)
        regB = sb("regB", [128, FC, TT], BF16)
        slots = [sb(f"slot{i}", [128, 8192], BF16) for i in range(3)]
        xgP = sb("xgP", [128, KC, TT], BF16)
        w3 = [sb(f"w3_{i}", [128, TT], F32) for i in range(4)]
        xst = [sb(f"xst{i}", [128, TT], F32) for i in range(2)]
        sct = [sb(f"sc{i}", [128, TT], F32) for i in range(NSC)]
        sqt = [sb(f"sq{i}", [128, TT], BF16) for i in range(NSQ)]
        sc_rs3 = sb("sc_rs3", [128, TT], F32)
        consts = sb("constsb", [128, NCONST], F32)
        poolw = sb("poolw", [128, 4, 2, 256], BF16)
        ones = sb("ones", [128, 128], BF16)
        carry_cu = sb("carry_cu", [128, 8, 2], F32)
        carry_v = sb("carry_v", [128, 8, 16], F32)
        xhalo = sb("xhalo", [128, KC, HALO], F32)
        xgh = sb("xgh", [128, KC, HALO], BF16)
        sqh = sb("sqh", [128, KC, HALO], BF16)
        small = sb("small", [128, 64], F32)
        banks = [es.enter_context(nc.psum_tensor(f"bank{i}", [128, TT], F32)) for i in range(8)]
        sems = {n: es.enter_context(nc.semaphore(n)) for n in sem_names}
        block = es.enter_context(nc.Block())

        P = Prog()
        rr = {"bank": 0, "sc": 0, "sq": 0}
        held = set()

        def next_bank(hold=False):
            for _ in range(9):
                b = rr["bank"]
                rr["bank"] = (b + 1) % 8
                if b not in held:
                    break
            else:
                raise RuntimeError("no free bank")
            if hold:
                held.add(b)
            return P.alloc("ps", b)

        def release(h):
            held.discard(h.i)

        def next_sc():
            i = rr["sc"]
            rr["sc"] = (i + 1) % NSC
            return P.alloc("sc", i)

        def next_sq():
            i = rr["sq"]
            rr["sq"] = (i + 1) % NSQ
            return P.alloc("sq", i)

        def cst(c):
            return consts[:, c:c + 1]

        def xg(k):
            return regA[:, k, :]

        def mixed(k):
            return regA[:, 16 + k, :]

        def ffv(i):
            return regA[:, 2 * i:2 * i + 2, :].bitcast(F32).rearrange("p a b -> p (a b)")

        def mov(i):
            return regB[:, 2 * i:2 * i + 2, :].bitcast(F32).rearrange("p a b -> p (a b)")

        def W5(i):
            return mov(i)

        def K5(i):
            return [("B", 2 * i), ("B", 2 * i + 1)]

        def W8(i):
            return regB[:, 32 + 3 * i:35 + 3 * i, :].bitcast(F32).rearrange("p a b -> p (a b)")[:, 0:528]

        def K8(i):
            return [("B", 32 + 3 * i + d) for d in range(3)]

        def PL(pj):
            return regB[:, 12 + pj, :]

        def KPL(pj):
            return [("B", 12 + pj)]

        def xgp(k):
            return xgP[:, k, :]

        stages = []
        for t in range(NTILE):
            for j in range(8):
                stages.append(("in", j))
            for j in range(4):
                stages.append(("out", j))
            for j in range(22):
                stages.append(("gu", j))
            for c2 in range(8):
                for h in range(2):
                    stages.append(("dn", c2, h))
        st_state = {"next_load": 0, "next_use": 0}

        def emit_load(n):
            st = stages[n]
            s = n % 3
            slot = slots[s]
            if st[0] == "dn":
                dst = slot[:, 0:5632].rearrange("p (a b) -> p a b", b=1408)
                src = w_dn_d[st[1] * 2 + st[2]].rearrange("p (a b) -> p a b", b=1408)
            else:
                srcd = {"in": w_in_d, "out": w_out_d, "gu": w_gu_d}[st[0]]
                dst = slot[:, 0:8192].rearrange("p (a b) -> p a b", b=2048)
                src = srcd[st[1]].rearrange("p (a b) -> p a b", b=2048)
            sk = f"s{s}"
            extra = []
            if n == 0:
                extra = [("x", k) for k in range(KC)]
            elif n in (1, 2):
                extra = [("slot", n - 1)]
            P.dma("pool", lambda e, dst=dst, src=src, sk=sk: e.dma_start(out=dst, in_=src).then_inc(sems[sk], 16),
                  sk, reads=extra, writes=[("slot", s)])

        def use_stage(kind):
            n = st_state["next_use"]
            assert stages[n][0] == kind, (stages[n], kind)
            st_state["next_use"] = n + 1
            while st_state["next_load"] < min(len(stages), n + 3):
                emit_load(st_state["next_load"])
                st_state["next_load"] += 1
            s = n % 3
            if kind == "dn":
                view = slots[s][:, 0:5632].rearrange("p (k n) -> p k n", n=256)
            else:
                view = slots[s][:, 0:8192].rearrange("p (k n) -> p k n", n=512)
            return s, view

        def mm_group(bank, lhs_fn, rhs_fn, nk, reads, c_lo=0, ncols=TT, start=True, stop=True, korder=None,
                     per_k_reads=None):
            ks = list(range(nk)) if korder is None else list(korder)
            if per_k_reads is not None:
                for n, k in enumerate(ks):
                    P.op("pe", lambda e, n=n, k=k: e.matmul(banks[bank.i][:, c_lo:c_lo + ncols], lhsT=lhs_fn(k), rhs=rhs_fn(k),
                                                           start=(start and n == 0), stop=(stop and n == nk - 1)),
                         reads=list(per_k_reads(k)) + list(reads), writes=[bank])
                return

            def fn(e):
                inst = None
                for n, k in enumerate(ks):
                    inst = e.matmul(banks[bank.i][:, c_lo:c_lo + ncols], lhsT=lhs_fn(k), rhs=rhs_fn(k),
                                    start=(start and n == 0), stop=(stop and n == nk - 1))
                return inst
            P.op("pe", fn, reads=reads, writes=[bank])

        def mm_multi(glist, ks, kreads, common, nk_total, done_before):
            for n, k in enumerate(ks):
                pos = done_before + n
                for (bank, lhs_fn, rhs_fn) in glist:
                    P.op("pe", lambda e, bank=bank, lhs_fn=lhs_fn, rhs_fn=rhs_fn, k=k, pos=pos: e.matmul(
                        banks[bank.i][:], lhsT=lhs_fn(k), rhs=rhs_fn(k), start=(pos == 0), stop=(pos == nk_total - 1)),
                         reads=list(kreads(k)) + list(common), writes=[bank])

        def rsqrt_from_bank(bank, n, ncols=TT, fixed=None):
            l = next_sc()
            P.op("act", lambda e: e.activation(out=sct[l.i][:, 0:ncols], in_=banks[bank.i][:, 0:ncols], func=AF.Ln,
                                               scale=1.0 / n, bias=EPS),
                 reads=[bank], writes=[l])
            if fixed is not None:
                P.op("act", lambda e: e.activation(out=fixed[:, 0:ncols], in_=sct[l.i][:, 0:ncols], func=AF.Exp, scale=-0.5),
                     reads=[l], writes=["rs3"])
                return None
            r = next_sc()
            P.op("act", lambda e: e.activation(out=sct[r.i][:, 0:ncols], in_=sct[l.i][:, 0:ncols], func=AF.Exp, scale=-0.5),
                 reads=[l], writes=[r])
            return r

        def act_copy(out, in_, reads, writes):
            P.op("act", lambda e: e.activation(out=out, in_=in_, func=AF.Copy), reads=reads, writes=writes)

        def act_square(out, in_, reads, writes):
            P.op("act", lambda e: e.activation(out=out, in_=in_, func=AF.Square), reads=reads, writes=writes)

        def stat_mm(bank, q, first, last_):
            P.op("pe", lambda e: e.matmul(banks[bank.i][:], lhsT=ones[:], rhs=sqt[q.i][:], start=first, stop=last_),
                 reads=[q, "ones"], writes=[bank])

        P.dma("sp", lambda e: e.dma_start(out=consts[:], in_=consts_d).then_inc(sems["misc"], 16), "misc",
              writes=["consts"])
        P.dma("pool", lambda e: e.dma_start(out=poolw[:], in_=pool_w_v).then_inc(sems["pw"], 16), "pw",
              writes=["poolw"])
        P.op("dve", lambda e: e.memset(ones[:], 1.0), writes=["ones"])

        def dump(idx, src, reads):
            if not debug:
                return
            P.dma("pool", lambda e: e.dma_start(out=dbg_d[idx], in_=src).then_inc(sems[f"d{idx}"], 16), f"d{idx}",
                  reads=reads)

        def emit_tile(t, prev_tail):
            c0 = HALO + t * TT
            last = (t == NTILE - 1)

            def load_xres():
                for q in range(4):
                    P.dma("sp", lambda e, q=q: e.dma_start(out=xres[:, 4 * q:4 * q + 4, :],
                                                           in_=xh_v[:, 4 * q:4 * q + 4, c0:c0 + TT]).then_inc(sems[f"x{q}"], 16),
                          f"x{q}", writes=[("x", k) for k in range(4 * q, 4 * q + 4)])
            if t == 0:
                load_xres()
                P.dma("sp", lambda e: e.dma_start(out=xhalo[:], in_=xh_v[:, :, 0:HALO]).then_inc(sems["xhl"], 16),
                      "xhl", writes=["xhalo"])
                sb0 = next_bank(hold=True)
                for kc in range(KC):
                    q = next_sq()
                    act_square(sqt[q.i][:], xres[:, kc, :], [("x", kc)], [q])
                    stat_mm(sb0, q, kc == 0, kc == KC - 1)
                r0 = rsqrt_from_bank(sb0, D)
                release(sb0)
                dump(0, sct[r0.i][:], [r0])
                for kc in range(KC):
                    P.op("dve", lambda e, kc=kc: e.scalar_tensor_tensor(out=xgp(kc), in0=xres[:, kc, :], scalar=cst(C_G0 + kc),
                                                                        in1=sct[r0.i][:], op0=ALU.mult, op1=ALU.mult),
                         reads=[("x", kc), r0, "consts"], writes=[("XP", kc)])
                dump(1, xgp(0), [("XP", 0)])
                dump(15, xgp(15), [("XP", 15)])
                act_square(sqh[:], xhalo[:], ["xhalo"], ["sqh"])
                sbh = next_bank()
                mm_group(sbh, lambda k: ones[:], lambda k: sqh[:, k, :], KC, reads=["sqh", "ones"], ncols=HALO)
                rh = rsqrt_from_bank(sbh, D, ncols=HALO)
                for kc in range(KC):
                    P.op("dve", lambda e, kc=kc: e.scalar_tensor_tensor(out=xgh[:, kc, :], in0=xhalo[:, kc, :],
                                                                        scalar=cst(C_G0 + kc), in1=sct[rh.i][:, 0:HALO],
                                                                        op0=ALU.mult, op1=ALU.mult),
                         reads=["xhalo", rh, "consts"], writes=["xgh"])

            def make_prefetch(tn):
                cn = HALO + tn * TT
                st = {"bank": None, "r": None, "n": 0}

                def load(kc):
                    i = st["n"] % 2
                    st["n"] += 1
                    P.dma("sp", lambda e: e.dma_start(out=xst[i][:], in_=xh_v[:, kc, cn:cn + TT]).then_inc(sems[f"xs{i}"], 16),
                          f"xs{i}", writes=[("xst", i)])
                    return i

                steps = []
                pending = {}

                def s_load(n):
                    def f():
                        pending[n] = load(n % KC)
                    return f

                def s_comp(n):
                    def f():
                        i = pending.pop(n)
                        kc = n % KC
                        if n < KC:
                            if n == 0:
                                st["bank"] = next_bank(hold=True)
                            q = next_sq()
                            act_square(sqt[q.i][:], xst[i][:], [("xst", i)], [q])
                            stat_mm(st["bank"], q, kc == 0, kc == KC - 1)
                            if kc == KC - 1:
                                st["r"] = rsqrt_from_bank(st["bank"], D)
                                release(st["bank"])
                        else:
                            r = st["r"]
                            P.op("dve", lambda e: e.scalar_tensor_tensor(out=xgp(kc), in0=xst[i][:], scalar=cst(C_G0 + kc),
                                                                         in1=sct[r.i][:], op0=ALU.mult, op1=ALU.mult),
                                 reads=[("xst", i), r, "consts"], writes=[("XP", kc)])
                    return f

                steps.append(s_load(0))
                for n in range(2 * KC):
                    if n + 1 < 2 * KC:
                        steps.append(s_load(n + 1))
                    steps.append(s_comp(n))
                return steps

            U = [dict() for _ in range(8)]
            xg_reads = [("XP", k) for k in range(KC)]

            def M(j):
                s, wv = use_stage("in")
                u = U[j]
                if t == 0 and j == 0:
                    gl = []
                    for name, col in (("C", 1), ("u", 2), ("v", 3), ("B", 0)):
                        b = next_bank()
                        u[name] = b
                        gl.append((b, (lambda k, col=col: wv[:, k, col * 128:(col + 1) * 128]), (lambda k: xgp(k))))
                    mm_multi(gl, range(KC), lambda k: [("XP", k)], [("slot", s)], KC, 0)
                else:
                    for name, col in (("C", 1), ("u", 2), ("v", 3), ("B", 0)):
                        b = next_bank()
                        u[name] = b
                        mm_group(b, lambda k, col=col: wv[:, k, col * 128:(col + 1) * 128], lambda k: xgp(k), KC,
                                 reads=xg_reads + [("slot", s)])
                if t == 0:
                    b = next_bank()
                    u["halo"] = b
                    for hi, col in enumerate((1, 2, 3)):
                        mm_group(b, lambda k, col=col: wv[:, k, col * 128:(col + 1) * 128], lambda k: xgh[:, k, :], KC,
                                 reads=["xgh", ("slot", s)], c_lo=16 * hi, ncols=HALO)

            def E(j):
                u = U[j]
                HQ = {"h": [], "q": []}
                cur_list = ["h"]

                def OP(eng, fn, reads, writes):
                    HQ[cur_list[0]].append((eng, fn, reads, writes))

                def ACOPY(out, in_, reads, writes):
                    OP("act", lambda e: e.activation(out=out, in_=in_, func=AF.Copy), reads, writes)
                hC, hp, hcu = K5(0), K5(1), K8(0)
                hy = K5(2 + j % 2)
                Csb, p, cu, y = W5(0), W5(1), W8(0), W5(2 + j % 2)
                u["y"] = 2 + j % 2
                bC, bu, bB, bv = u["C"], u["u"], u["B"], u["v"]
                ACOPY(Csb, banks[bC.i][:], [bC], hC)
                OP("dve", lambda e: e.tensor_tensor(out=cu[:, 2:2 + TT], in0=Csb, in1=banks[bu.i][:], op=ALU.mult),
                     reads=hC + [bu], writes=hcu)
                if t == 0:
                    bh = u["halo"]
                    ACOPY(small[:, 0:2], banks[bh.i][:, 14:16], [bh], ["small"])
                    OP("dve", lambda e: e.tensor_tensor(out=cu[:, 0:2], in0=small[:, 0:2], in1=banks[bh.i][:, 30:32],
                                                          op=ALU.mult),
                         reads=["small", bh], writes=hcu)
                else:
                    ACOPY(cu[:, 0:2], carry_cu[:, j, :], [("ccu", j)], hcu)
                if not last:
                    ACOPY(carry_cu[:, j, :], cu[:, TT:TT + 2], hcu, [("ccu", j)])
                OP("dve", lambda e: e.tensor_scalar(out=p, in0=cu[:, 2:2 + TT], scalar1=cst(C_CONV + 2 * 8 + j),
                                                      scalar2=None, op0=ALU.mult),
                     reads=hcu + ["consts"], writes=hp)
                OP("dve", lambda e: e.scalar_tensor_tensor(out=p, in0=cu[:, 1:1 + TT], scalar=cst(C_CONV + 1 * 8 + j),
                                                             in1=p, op0=ALU.mult, op1=ALU.add),
                     reads=hcu + hp + ["consts"], writes=hp)
                OP("dve", lambda e: e.scalar_tensor_tensor(out=p, in0=cu[:, 0:TT], scalar=cst(C_CONV + 0 * 8 + j),
                                                             in1=p, op0=ALU.mult, op1=ALU.add),
                     reads=hcu + hp + ["consts"], writes=hp)
                OP("dve", lambda e: e.tensor_tensor(out=y, in0=p, in1=banks[bB.i][:], op=ALU.mult),
                     reads=hp + [bB], writes=hy)
                q = next_sq()
                u["ysq"] = q
                OP("act", lambda e: e.activation(out=sqt[q.i][:], in_=y, func=AF.Square), hy, [q])
                cur_list[0] = "q"
                hv, hA, hB = K8(1), K8(2), K8(3)
                vb, sA, sB = W8(1), W8(2), W8(3)
                g = j // 2
                wwin = 2 << g
                ACOPY(vb[:, 16:16 + TT], banks[bv.i][:], [bv], hv)
                if t == 0:
                    ACOPY(vb[:, 0:16], banks[bh.i][:, 32:48], [bh], hv)
                else:
                    ACOPY(vb[:, 0:16], carry_v[:, j, :], [("cv", j)], hv)
                if not last:
                    ACOPY(carry_v[:, j, :], vb[:, TT:TT + 16], hv, [("cv", j)])
                OP("dve", lambda e: e.tensor_tensor(out=sA[:, 1:528], in0=vb[:, 1:528], in1=vb[:, 0:527], op=ALU.add),
                     reads=hv, writes=hA)
                cur, hcur = sA, hA
                if g >= 1:
                    OP("dve", lambda e: e.tensor_tensor(out=sB[:, 3:528], in0=sA[:, 3:528], in1=sA[:, 1:526], op=ALU.add),
                         reads=hA, writes=hB)
                    cur, hcur = sB, hB
                if g >= 2:
                    OP("dve", lambda e: e.tensor_tensor(out=sA[:, 7:528], in0=sB[:, 7:528], in1=sB[:, 3:524], op=ALU.add),
                         reads=hB, writes=hA)
                    cur, hcur = sA, hA
                if g >= 3:
                    OP("dve", lambda e: e.tensor_tensor(out=sB[:, 15:528], in0=sA[:, 15:528], in1=sA[:, 7:520], op=ALU.add),
                         reads=hA, writes=hB)
                    cur, hcur = sB, hB
                pj = j % 4
                OP("dve", lambda e: e.scalar_tensor_tensor(out=PL(pj), in0=cur[:, 16:16 + TT], scalar=1.0 / wwin,
                                                             in1=vb[:, 16:16 + TT], op0=ALU.mult, op1=ALU.subtract),
                     reads=hcur + hv, writes=KPL(pj))
                if t == 0:
                    OP("dve", lambda e: e.tensor_tensor(out=small[:, 16:32], in0=cur[:, 16:32],
                                                          in1=consts[:, C_INV + g * 16:C_INV + (g + 1) * 16], op=ALU.mult),
                         reads=hcur + ["consts"], writes=["small2"])
                    OP("dve", lambda e: e.tensor_tensor(out=PL(pj)[:, 0:16], in0=small[:, 16:32], in1=vb[:, 16:32],
                                                          op=ALU.subtract),
                         reads=["small2"] + hv, writes=KPL(pj))
                hl, ql = HQ["h"], HQ["q"]
                for n in range(max(len(hl), len(ql))):
                    if n < len(hl):
                        P.op(hl[n][0], hl[n][1], reads=hl[n][2], writes=hl[n][3])
                    if n < len(ql):
                        P.op(ql[n][0], ql[n][1], reads=ql[n][2], writes=ql[n][3])
                if t == 0 and j == 0:
                    dump(2, cu[:, 2:2 + TT], hcu)
                    dump(14, Csb, hC)
                    dump(3, y, hy)
                    dump(5, PL(pj), KPL(pj))

            def S(j):
                u = U[j]
                b = next_bank()
                u["hst"] = b
                stat_mm(b, u["ysq"], True, True)
                if j % 2 == 1:
                    g = j // 2
                    u["yp"] = []
                    for e2 in range(2):
                        b2 = next_bank()
                        u["yp"].append(b2)
                        mm_group(b2, lambda k, e2=e2: poolw[:, g, k, e2 * 128:(e2 + 1) * 128],
                                 lambda k: PL((2 * g + k) % 4), 2,
                                 reads=KPL((2 * g) % 4) + KPL((2 * g + 1) % 4) + ["poolw"])

            def F(j):
                u = U[j]
                r = rsqrt_from_bank(u["hst"], 128)
                iy = u["y"]
                P.op("dve", lambda e: e.tensor_tensor(out=mixed(j), in0=W5(iy), in1=sct[r.i][:], op=ALU.mult),
                     reads=K5(iy) + [r], writes=[("A", 16 + j)])
                if t == 0 and j == 0:
                    dump(4, mixed(0), [("A", 16)])
                if j % 2 == 1:
                    u["yps"] = []
                    u["ypsq"] = []
                    for e2 in range(2):
                        iw = 4 + e2
                        q = next_sq()
                        b2 = u["yp"][e2]
                        u["yps"].append(iw)
                        u["ypsq"].append(q)
                        act_copy(W5(iw), banks[b2.i][:], [b2], K5(iw))
                        act_square(sqt[q.i][:], banks[b2.i][:], [b2], [q])

            def S2(j):
                if j % 2 != 1:
                    return
                u = U[j]
                b = next_bank()
                u["pst"] = b
                for e2 in range(2):
                    stat_mm(b, u["ypsq"][e2], e2 == 0, e2 == 1)

            def F2(j):
                if j % 2 != 1:
                    return
                u = U[j]
                g = j // 2
                r = rsqrt_from_bank(u["pst"], 256)
                for e2 in range(2):
                    iw = u["yps"][e2]
                    c = 2 * g + e2
                    P.op("dve", lambda e, iw=iw, c=c: e.scalar_tensor_tensor(out=mixed(8 + c), in0=W5(iw),
                                                                             scalar=cst(C_PSC + c), in1=sct[r.i][:],
                                                                             op0=ALU.mult, op1=ALU.mult),
                         reads=K5(iw) + [r, "consts"], writes=[("A", 16 + 8 + c)])
                if t == 0 and j == 1:
                    dump(6, mixed(8), [("A", 24)])

            OUT_EARLY = [0, 1, 2, 3, 4, 5, 6, 8, 9, 10, 11, 12, 13]
            OUT_LATE = [7, 14, 15]
            op0 = {}

            def out_stage0_early():
                s0, wv0 = use_stage("out")
                op0["s"], op0["wv"] = s0, wv0
                op0["banks"] = [next_bank(hold=True) for _ in range(4)]
                op0["gl"] = [(op0["banks"][i], (lambda k, i=i: wv0[:, k, i * 128:(i + 1) * 128]), (lambda k: mixed(k)))
                             for i in range(4)]
                mm_multi(op0["gl"], OUT_EARLY, lambda k: [("A", 16 + k)], [("slot", s0)], KC, 0)

            for step in range(8 + 2):
                if step < 8:
                    M(step)
                if step == 8:
                    out_stage0_early()
                if 0 <= step - 1 < 8:
                    S(step - 1)
                if 0 <= step - 2 < 8:
                    S2(step - 2)
                if step < 8:
                    E(step)
                if step < 8 and prev_tail:
                    for st_ in prev_tail[2 * step:2 * step + 2]:
                        st_()
                    if step == 7:
                        load_xres()
                if 0 <= step - 1 < 8:
                    F(step - 1)
                if 0 <= step - 2 < 8:
                    F2(step - 2)

            mixed_reads = [("A", 16 + k) for k in range(KC)]
            sb1 = next_bank(hold=True)
            pend = None

            def evac_mo(i, b):
                q = next_sq()
                act_square(sqt[q.i][:], banks[b.i][:], [b], [q])
                stat_mm(sb1, q, i == 0, i == KC - 1)
                act_copy(mov(i), banks[b.i][:], [b], [("B", 2 * i), ("B", 2 * i + 1)])

            mm_multi(op0["gl"], OUT_LATE, lambda k: [("A", 16 + k)], [("slot", op0["s"])], KC, len(OUT_EARLY))
            for b in op0["banks"]:
                release(b)
            for i in range(3):
                evac_mo(i, op0["banks"][i])
            pend = (3, op0["banks"][3])
            for i in range(4, KC):
                if i % 4 == 0:
                    s, wv = use_stage("out")
                b = next_bank()
                mm_group(b, lambda k, i=i, wv=wv: wv[:, k, (i % 4) * 128:(i % 4 + 1) * 128], lambda k: mixed(k), KC,
                         reads=mixed_reads + [("slot", s)])
                if pend is not None:
                    evac_mo(*pend)
                pend = (i, b)
            evac_mo(*pend)
            rs1 = rsqrt_from_bank(sb1, D)
            release(sb1)
            sb2 = next_bank(hold=True)
            s_g0, wv_g0 = use_stage("gu")
            gb = [next_bank() for _ in range(4)]
            gcols = [0, 256, 128, 384]
            gl0 = [(gb[n], (lambda k, c=gcols[n]: wv_g0[:, k, c:c + 128]), (lambda k: xg(k))) for n in range(4)]
            BURST_AFTER = {5: range(0, 6), 9: range(6, 10), 12: range(10, 13), 14: range(13, 15), 15: range(15, 16)}
            def x1_stt(i):
                P.op("dve", lambda e, i=i: e.scalar_tensor_tensor(out=mov(i), in0=mov(i), scalar=cst(C_G1 + i),
                                                                  in1=sct[rs1.i][:], op0=ALU.mult, op1=ALU.mult),
                     reads=[("B", 2 * i), ("B", 2 * i + 1), rs1, "consts"], writes=[("B", 2 * i), ("B", 2 * i + 1)])

            x1_stt(0)
            for i in range(KC):
                if i + 1 < KC:
                    x1_stt(i + 1)
                P.op("dve", lambda e, i=i: e.tensor_tensor(out=xres[:, i, :], in0=mov(i), in1=xres[:, i, :], op=ALU.add),
                     reads=[("B", 2 * i), ("B", 2 * i + 1), ("x", i)], writes=[("x", i)])
                q = next_sq()
                act_square(sqt[q.i][:], xres[:, i, :], [("x", i)], [q])
                stat_mm(sb2, q, i == 0, i == KC - 1)
                P.op("act", lambda e, i=i: e.activation(out=xg(i), in_=xres[:, i, :], func=AF.Copy, scale=cst(C_G2 + i)),
                     reads=[("x", i), "consts"], writes=[("A", i)])
                if i in BURST_AFTER:
                    ks = list(BURST_AFTER[i])
                    mm_multi(gl0, ks, lambda k: [("A", k)], [("slot", s_g0)], KC, ks[0])
            r2 = rsqrt_from_bank(sb2, D)
            release(sb2)
            if t == 0:
                dump(7, xres[:, 0, :], [("x", 0)])
                dump(11, sct[rs1.i][:], [rs1])
                dump(12, sct[r2.i][:], [r2])

            hf_reads = [("A", k) for k in range(KC)]
            pend = None
            sgi = [0]

            pf_steps = make_prefetch(t + 1) if not last else []
            pf_pos = [0]

            def pf_advance(n):
                for _ in range(n):
                    if pf_pos[0] < len(pf_steps):
                        pf_steps[pf_pos[0]]()
                        pf_pos[0] += 1

            def evac_gu(f, bg, bu):
                hg = P.alloc("w3", 2 * (sgi[0] % 2))
                hu = P.alloc("w3", 2 * (sgi[0] % 2) + 1)
                sgi[0] += 1
                tg, tu = w3[hg.i], w3[hu.i]
                P.op("dve", lambda e: e.tensor_tensor(out=tg[:], in0=banks[bg.i][:], in1=sct[r2.i][:], op=ALU.mult),
                     reads=[bg, r2], writes=[hg])
                P.op("dve", lambda e: e.tensor_tensor(out=tu[:], in0=banks[bu.i][:], in1=sct[r2.i][:], op=ALU.mult),
                     reads=[bu, r2], writes=[hu])
                P.op("act", lambda e: e.activation(out=tg[:], in_=tg[:], func=AF.Silu),
                     reads=[hg], writes=[hg])
                P.op("dve", lambda e: e.tensor_tensor(out=regB[:, f, :], in0=tg[:], in1=tu[:], op=ALU.mult),
                     reads=[hg, hu], writes=[("B", f)])

            evac_gu(0, gb[0], gb[1])
            pend = (1, gb[2], gb[3])
            pf_advance(4)
            for f in range(2, FC):
                if f % 2 == 0:
                    s, wv = use_stage("gu")
                e2 = f % 2
                bg = next_bank()
                mm_group(bg, lambda k, wv=wv, e2=e2: wv[:, k, e2 * 128:(e2 + 1) * 128], lambda k: xg(k), KC,
                         reads=hf_reads + [("slot", s)])
                bu = next_bank()
                mm_group(bu, lambda k, wv=wv, e2=e2: wv[:, k, 256 + e2 * 128:256 + (e2 + 1) * 128], lambda k: xg(k), KC,
                         reads=hf_reads + [("slot", s)])
                if pend is not None:
                    evac_gu(*pend)
                pend = (f, bg, bu)
                pf_advance(2)
            evac_gu(*pend)
            pf_advance(len(pf_steps))
            if t == 0:
                dump(9, regB[:, 0, :], [("B", 0)])

            sb3 = next_bank(hold=True)
            pend = []

            def evac_ff(i, b):
                q = next_sq()
                act_square(sqt[q.i][:], banks[b.i][:], [b], [q])
                stat_mm(sb3, q, i == 0, i == KC - 1)
                act_copy(ffv(i), banks[b.i][:], [b], [("A", 2 * i), ("A", 2 * i + 1)])

            for c2 in range(8):
                bb = [next_bank(), next_bank()]
                for h in range(2):
                    s, wv = use_stage("dn")
                    for e2 in range(2):
                        mm_group(bb[e2], lambda k, wv=wv, e2=e2: wv[:, k, e2 * 128:(e2 + 1) * 128],
                                 lambda k, h=h: regB[:, h * 22 + k, :], 22,
                                 reads=[("B", h * 22 + k) for k in range(22)] + [("slot", s)],
                                 start=(h == 0), stop=(h == 1))
                for (i, b) in pend:
                    evac_ff(i, b)
                pend = [(2 * c2, bb[0]), (2 * c2 + 1, bb[1])]
            for (i, b) in pend:
                evac_ff(i, b)
            rsqrt_from_bank(sb3, D, fixed=sc_rs3)
            release(sb3)
            if t == 0:
                dump(10, ffv(0), [("A", 0), ("A", 1)])
                dump(13, sc_rs3[:], ["rs3"])

            def tail_step(i):
                P.op("dve", lambda e: e.scalar_tensor_tensor(out=ffv(i), in0=ffv(i), scalar=cst(C_G3 + i),
                                                             in1=sc_rs3[:], op0=ALU.mult, op1=ALU.mult),
                     reads=[("A", 2 * i), ("A", 2 * i + 1), "rs3", "consts"], writes=[("A", 2 * i), ("A", 2 * i + 1)])
                P.op("dve", lambda e: e.tensor_tensor(out=xres[:, i, :], in0=ffv(i), in1=xres[:, i, :], op=ALU.add),
                     reads=[("A", 2 * i), ("A", 2 * i + 1), ("x", i)], writes=[("x", i)])
                if i % 4 == 3:
                    q = i // 4
                    P.dma("sp", lambda e: e.dma_start(out=yT_v[:, 4 * q:4 * q + 4, t * TT:(t + 1) * TT],
                                                      in_=xres[:, 4 * q:4 * q + 4, :]).then_inc(sems[f"o{q}"], 16),
                          f"o{q}", reads=[("x", k) for k in range(4 * q, 4 * q + 4)])

            order = [8, 12, 9, 13, 10, 14, 11, 15, 0, 1, 2, 3, 4, 5, 6, 7]
            steps = [(lambda i=i: tail_step(i)) for i in order]
            if last:
                for st_ in steps:
                    st_()
                return []
            return steps

        tail = []
        for t in range(NTILE):
            tail = emit_tile(t, tail)
        P.wait_all("sp", [(f"o{q}", P.dma_cnt[f"o{q}"]) for q in range(4)])

        @block.sync
        def _(e):
            P.replay("sp", e, sems)

        @block.gpsimd
        def _(e):
            P.replay("pool", e, sems)

        @block.tensor
        def _(e):
            P.replay("pe", e, sems)

        @block.scalar
        def _(e):
            P.replay("act", e, sems)

        @block.vector
        def _(e):
            P.replay("dve", e, sems)

    return nc


_PROGRAM = None


def _host_layout(x, ln_mix_pre, w_in, conv_w, pool_w, pool_scale, w_out, ln_mix_post,
                 ln_ffn_pre, w_gate, w_up, w_down, ln_ffn_post):
    f32 = np.float32
    x = np.asarray(x, f32)
    w_in = np.asarray(w_in, f32)[0]
    w_in_p = w_in.reshape(D, 4, 8, 128).transpose(0, 2, 1, 3).reshape(D, 4096)
    wg = np.asarray(w_gate, f32)[0].reshape(D, 22, 256)
    wu = np.asarray(w_up, f32)[0].reshape(D, 22, 256)
    w_gu = np.stack([wg, wu], axis=2).reshape(D, 2 * DFF)

    def stage_major(w, ncol):
        K, N = w.shape
        return np.ascontiguousarray(
            w.reshape(K // 128, 128, N // ncol, ncol).transpose(2, 1, 0, 3).reshape(N // ncol, 128, (K // 128) * ncol))

    wst_in = stage_major(w_in_p, 512)
    wst_out = stage_major(np.asarray(w_out, f32)[0], 512)
    wst_gu = stage_major(w_gu, 512)
    wd = np.asarray(w_down, f32)[0]
    wst_dn = np.ascontiguousarray(
        wd.reshape(2, 22, 128, 8, 256).transpose(3, 0, 2, 1, 4).reshape(16, 128, 22 * 256))
    pool_w_ = np.ascontiguousarray(np.asarray(pool_w, f32)[0])

    def pk(v):
        return np.asarray(v, f32).reshape(-1, 128).T

    base = np.zeros((128, NCONST), f32)
    base[:, C_G0:C_G0 + 16] = pk(ln_mix_pre[0])
    base[:, C_G1:C_G1 + 16] = pk(ln_mix_post[0])
    base[:, C_G2:C_G2 + 16] = pk(ln_ffn_pre[0])
    base[:, C_G3:C_G3 + 16] = pk(ln_ffn_post[0])
    cw = np.asarray(conv_w, f32)[0]
    for j in range(3):
        base[:, C_CONV + j * 8:C_CONV + (j + 1) * 8] = cw[j].reshape(8, 128).T
    base[:, C_PSC:C_PSC + 8] = np.asarray(pool_scale, f32)[0].reshape(8, 128).T

    in_maps = []
    for c in range(N_CORES):
        b, q = divmod(c, 4)
        xh = np.zeros((D, HALO + TOK), f32)
        lo = q * TOK - HALO
        if lo >= 0:
            xh[:, :] = x[b, lo:(q + 1) * TOK, :].T
        else:
            xh[:, HALO:] = x[b, 0:TOK, :].T
        cst = base.copy()
        for g, w in enumerate((2, 4, 8, 16)):
            for i in range(16):
                cnt = min(i + 1, w) if q == 0 else w
                cst[:, C_INV + g * 16 + i] = 1.0 / cnt
        in_maps.append({"xh": xh, "consts": cst, "wst_in": wst_in, "pool_w": pool_w_, "wst_out": wst_out,
                        "wst_gu": wst_gu, "wst_dn": wst_dn})
    return in_maps


def kernel(x, ln_mix_pre, w_in, conv_w, pool_w, pool_scale, w_out, ln_mix_post,
           ln_ffn_pre, w_gate, w_up, w_down, ln_ffn_post):
    global _PROGRAM
    in_maps = _host_layout(x, ln_mix_pre, w_in, conv_w, pool_w, pool_scale, w_out, ln_mix_post,
                           ln_ffn_pre, w_gate, w_up, w_down, ln_ffn_post)
    if _PROGRAM is None:
        _PROGRAM = build_program()
    res = run_bass_kernel_spmd(_PROGRAM, in_maps, core_ids=list(range(N_CORES)))
    out = np.empty((2, 4 * TOK, D), np.float32)
    for c in range(N_CORES):
        b, q = divmod(c, 4)
        out[b, q * TOK:(q + 1) * TOK, :] = np.asarray(res.results[c]["yT"], np.float32).T
    return out
```

```python
import numpy as np
import concourse.bass as bass
import concourse.mybir as mybir
from concourse.bass_utils import run_bass_kernel_spmd

F32 = mybir.dt.float32
BF16 = mybir.dt.bfloat16
AF = mybir.ActivationFunctionType
ALU = mybir.AluOpType

N_CORES = 8
D = 2048
KC = 16
TT = 512
NTILE = 4
TOK = 2048
HALO = 16
DFF = 5632
FC = 44
EPS = 1e-6
NCONST = 160
C_G0, C_G1, C_G2, C_G3, C_CONV, C_PSC, C_INV = 0, 16, 32, 48, 64, 88, 96
NSC = 4
NSQ = 4
SAME_ENGINE_SYNC = True

ENGS = ("pe", "act", "dve", "pool", "sp")


class R:
    __slots__ = ("kind", "i", "gen")

    def __init__(self, kind, i, gen):
        self.kind, self.i, self.gen = kind, i, gen


class Prog:
    def __init__(self):
        self.gen = {}
        self.ops = {e: [] for e in ENGS}
        self.cnt = {e: 0 for e in ENGS}
        self.res = {}
        self.waited = {e: {} for e in ENGS}
        self.dma_cnt = {}

    def alloc(self, kind, i):
        g = self.gen.get((kind, i), 0) + 1
        self.gen[(kind, i)] = g
        return R(kind, i, g)

    def _norm(self, lst):
        out = []
        for r in lst:
            if isinstance(r, R):
                assert self.gen[(r.kind, r.i)] == r.gen, ("stale handle", r.kind, r.i, r.gen, self.gen[(r.kind, r.i)])
                out.append((r.kind, r.i))
            else:
                out.append(r)
        return out

    def _collect(self, eng, reads, writes):
        need = {}

        def add(tok):
            if tok is None:
                return
            k, v = tok
            if k == eng and (eng == "pe" or not SAME_ENGINE_SYNC):
                return
            if need.get(k, 0) < v:
                need[k] = v

        for r in reads:
            st = self.res.get(r)
            if st is not None:
                add(st["w"])
        for w in writes:
            st = self.res.get(w)
            if st is not None:
                add(st["w"])
                for tk in st["r"]:
                    add(tk)
        waits = []
        wd = self.waited[eng]
        for k, v in need.items():
            if wd.get(k, 0) >= v:
                continue
            wd[k] = v
            waits.append((k, v))
        return waits

    def _update(self, tok, reads, writes):
        for r in reads:
            st = self.res.setdefault(r, {"w": None, "r": []})
            st["r"].append(tok)
        for w in writes:
            self.res[w] = {"w": tok, "r": []}

    def op(self, eng, fn, reads=(), writes=()):
        reads, writes = self._norm(reads), self._norm(writes)
        waits = self._collect(eng, reads, writes)
        self.cnt[eng] += 1
        tok = (eng, self.cnt[eng])
        self.ops[eng].append((waits, fn, True))
        self._update(tok, reads, writes)
        return tok

    def dma(self, eng, fn, semkey, reads=(), writes=(), n=1):
        reads, writes = self._norm(reads), self._norm(writes)
        waits = self._collect(eng, reads, writes)
        self.dma_cnt[semkey] = self.dma_cnt.get(semkey, 0) + 16 * n
        tok = (semkey, self.dma_cnt[semkey])
        self.ops[eng].append((waits, fn, False))
        self._update(tok, reads, writes)
        return tok

    def wait_all(self, eng, toks):
        waits = []
        for k, v in toks:
            if self.waited[eng].get(k, 0) < v:
                self.waited[eng][k] = v
                waits.append((k, v))
        self.ops[eng].append((waits, None, False))

    def replay(self, eng, e, sems):
        for waits, fn, signal in self.ops[eng]:
            for k, v in waits:
                e.wait_ge(sems[k], v)
            if fn is None:
                continue
            inst = fn(e)
            if signal:
                inst.then_inc(sems[eng], 1)


NDBG = 16


def build_program(debug=False):
    nc = bass.Bass("TRN2", target_bir_lowering=False)
    dbg_d = nc.dram_tensor("dbg", [NDBG, 128, TT], F32, kind="ExternalOutput").ap() if debug else None
    xh = nc.dram_tensor("xh", [D, HALO + TOK], F32, kind="ExternalInput").ap()
    consts_d = nc.dram_tensor("consts", [128, NCONST], F32, kind="ExternalInput").ap()
    w_in_d = nc.dram_tensor("wst_in", [8, 128, 8192], F32, kind="ExternalInput").ap()
    pool_w_d = nc.dram_tensor("pool_w", [4, 256, 256], F32, kind="ExternalInput").ap()
    w_out_d = nc.dram_tensor("wst_out", [4, 128, 8192], F32, kind="ExternalInput").ap()
    w_gu_d = nc.dram_tensor("wst_gu", [22, 128, 8192], F32, kind="ExternalInput").ap()
    w_dn_d = nc.dram_tensor("wst_dn", [16, 128, 5632], F32, kind="ExternalInput").ap()
    yT = nc.dram_tensor("yT", [D, TOK], F32, kind="ExternalOutput").ap()

    xh_v = xh.rearrange("(kc p) t -> p kc t", p=128)
    yT_v = yT.rearrange("(kc p) t -> p kc t", p=128)
    pool_w_v = pool_w_d.rearrange("g (kc p) j -> p g kc j", p=128)

    sem_names = [f"d{i}" for i in range(NDBG if debug else 0)] + list(ENGS) + ["s0", "s1", "s2", "x0", "x1", "x2", "x3", "o0", "o1", "o2", "o3", "misc", "pw", "xhl", "xs0", "xs1"]

    from contextlib import ExitStack
    with ExitStack() as es:
        def sb(name, shape, dt):
            return es.enter_context(nc.sbuf_tensor(name, shape, dt))

        xres = sb("xres", [128, KC, TT], F32)
        regA = sb("regA", [128, 32, TT], BF16)
        regB = sb("regB", [128, FC, TT], BF16)
        slots = [sb(f"slot{i}", [128, 8192], BF16) for i in range(3)]
        xgP = sb("xgP", [128, KC, TT], BF16)
        w3 = [sb(f"w3_{i}", [128, TT], F32) for i in range(4)]
        xst = [sb(f"xst{i}", [128, TT], F32) for i in range(2)]
        sct = [sb(f"sc{i}", [128, TT], F32) for i in range(NSC)]
        sqt = [sb(f"sq{i}", [128, TT], BF16) for i in range(NSQ)]
        sc_rs3 = sb("sc_rs3", [128, TT], F32)
        consts = sb("constsb", [128, NCONST], F32)
        poolw = sb("poolw", [128, 4, 2, 256], BF16)
        ones = sb("ones", [128, 128], BF16)
        carry_cu = sb("carry_cu", [128, 8, 2], F32)
        carry_v = sb("carry_v", [128, 8, 16], F32)
        xhalo = sb("xhalo", [128, KC, HALO], F32)
        xgh = sb("xgh", [128, KC, HALO], BF16)
        sqh = sb("sqh", [128, KC, HALO], BF16)
        small = sb("small", [128, 64], F32)
        banks = [es.enter_context(nc.psum_tensor(f"bank{i}", [128, TT], F32)) for i in range(8)]
        sems = {n: es.enter_context(nc.semaphore(n)) for n in sem_names}
        block = es.enter_context(nc.Block())

        P = Prog()
        rr = {"bank": 0, "sc": 0, "sq": 0}
        held = set()

        def next_bank(hold=False):
            for _ in range(9):
                b = rr["bank"]
                rr["bank"] = (b + 1) % 8
                if b not in held:
                    break
            else:
                raise RuntimeError("no free bank")
            if hold:
                held.add(b)
            return P.alloc("ps", b)

        def release(h):
            held.discard(h.i)

        def next_sc():
            i = rr["sc"]
            rr["sc"] = (i + 1) % NSC
            return P.alloc("sc", i)

        def next_sq():
            i = rr["sq"]
            rr["sq"] = (i + 1) % NSQ
            return P.alloc("sq", i)

        def cst(c):
            return consts[:, c:c + 1]

        def xg(k):
            return regA[:, k, :]

        def mixed(k):
            return regA[:, 16 + k, :]

        def ffv(i):
            return regA[:, 2 * i:2 * i + 2, :].bitcast(F32).rearrange("p a b -> p (a b)")

        def mov(i):
            return regB[:, 2 * i:2 * i + 2, :].bitcast(F32).rearrange("p a b -> p (a b)")

        def W5(i):
            return mov(i)

        def K5(i):
            return [("B", 2 * i), ("B", 2 * i + 1)]

        def W8(i):
            return regB[:, 32 + 3 * i:35 + 3 * i, :].bitcast(F32).rearrange("p a b -> p (a b)")[:, 0:528]

        def K8(i):
            return [("B", 32 + 3 * i + d) for d in range(3)]

        def PL(pj):
            return regB[:, 12 + pj, :]

        def KPL(pj):
            return [("B", 12 + pj)]

        def xgp(k):
            return xgP[:, k, :]

        stages = []
        for t in range(NTILE):
            for j in range(8):
                stages.append(("in", j))
            for j in range(4):
                stages.append(("out", j))
            for j in range(22):
                stages.append(("gu", j))
            for c2 in range(8):
                for h in range(2):
                    stages.append(("dn", c2, h))
        st_state = {"next_load": 0, "next_use": 0}

        def emit_load(n):
            st = stages[n]
            s = n % 3
            slot = slots[s]
            if st[0] == "dn":
                dst = slot[:, 0:5632].rearrange("p (a b) -> p a b", b=1408)
                src = w_dn_d[st[1] * 2 + st[2]].rearrange("p (a b) -> p a b", b=1408)
            else:
                srcd = {"in": w_in_d, "out": w_out_d, "gu": w_gu_d}[st[0]]
                dst = slot[:, 0:8192].rearrange("p (a b) -> p a b", b=2048)
                src = srcd[st[1]].rearrange("p (a b) -> p a b", b=2048)
            sk = f"s{s}"
            extra = []
            if n == 0:
                extra = [("x", k) for k in range(KC)]
            elif n in (1, 2):
                extra = [("slot", n - 1)]
            P.dma("pool", lambda e, dst=dst, src=src, sk=sk: e.dma_start(out=dst, in_=src).then_inc(sems[sk], 16),
                  sk, reads=extra, writes=[("slot", s)])

        def use_stage(kind):
            n = st_state["next_use"]
            assert stages[n][0] == kind, (stages[n], kind)
            st_state["next_use"] = n + 1
            while st_state["next_load"] < min(len(stages), n + 3):
                emit_load(st_state["next_load"])
                st_state["next_load"] += 1
            s = n % 3
            if kind == "dn":
                view = slots[s][:, 0:5632].rearrange("p (k n) -> p k n", n=256)
            else:
                view = slots[s][:, 0:8192].rearrange("p (k n) -> p k n", n=512)
            return s, view

        def mm_group(bank, lhs_fn, rhs_fn, nk, reads, c_lo=0, ncols=TT, start=True, stop=True, korder=None,
                     per_k_reads=None):
            ks = list(range(nk)) if korder is None else list(korder)
            if per_k_reads is not None:
                for n, k in enumerate(ks):
                    P.op("pe", lambda e, n=n, k=k: e.matmul(banks[bank.i][:, c_lo:c_lo + ncols], lhsT=lhs_fn(k), rhs=rhs_fn(k),
                                                           start=(start and n == 0), stop=(stop and n == nk - 1)),
                         reads=list(per_k_reads(k)) + list(reads), writes=[bank])
                return

            def fn(e):
                inst = None
                for n, k in enumerate(ks):
                    inst = e.matmul(banks[bank.i][:, c_lo:c_lo + ncols], lhsT=lhs_fn(k), rhs=rhs_fn(k),
                                    start=(start and n == 0), stop=(stop and n == nk - 1))
                return inst
            P.op("pe", fn, reads=reads, writes=[bank])

        def mm_multi(glist, ks, kreads, common, nk_total, done_before):
            for n, k in enumerate(ks):
                pos = done_before + n
                for (bank, lhs_fn, rhs_fn) in glist:
                    P.op("pe", lambda e, bank=bank, lhs_fn=lhs_fn, rhs_fn=rhs_fn, k=k, pos=pos: e.matmul(
                        banks[bank.i][:], lhsT=lhs_fn(k), rhs=rhs_fn(k), start=(pos == 0), stop=(pos == nk_total - 1)),
                         reads=list(kreads(k)) + list(common), writes=[bank])

        def rsqrt_from_bank(bank, n, ncols=TT, fixed=None):
            l = next_sc()
            P.op("act", lambda e: e.activation(out=sct[l.i][:, 0:ncols], in_=banks[bank.i][:, 0:ncols], func=AF.Ln,
                                               scale=1.0 / n, bias=EPS),
                 reads=[bank], writes=[l])
            if fixed is not None:
                P.op("act", lambda e: e.activation(out=fixed[:, 0:ncols], in_=sct[l.i][:, 0:ncols], func=AF.Exp, scale=-0.5),
                     reads=[l], writes=["rs3"])
                return None
            r = next_sc()
            P.op("act", lambda e: e.activation(out=sct[r.i][:, 0:ncols], in_=sct[l.i][:, 0:ncols], func=AF.Exp, scale=-0.5),
                 reads=[l], writes=[r])
            return r

        def act_copy(out, in_, reads, writes):
            P.op("act", lambda e: e.activation(out=out, in_=in_, func=AF.Copy), reads=reads, writes=writes)

        def act_square(out, in_, reads, writes):
            P.op("act", lambda e: e.activation(out=out, in_=in_, func=AF.Square), reads=reads, writes=writes)

        def stat_mm(bank, q, first, last_):
            P.op("pe", lambda e: e.matmul(banks[bank.i][:], lhsT=ones[:], rhs=sqt[q.i][:], start=first, stop=last_),
                 reads=[q, "ones"], writes=[bank])

        P.dma("sp", lambda e: e.dma_start(out=consts[:], in_=consts_d).then_inc(sems["misc"], 16), "misc",
              writes=["consts"])
        P.dma("pool", lambda e: e.dma_start(out=poolw[:], in_=pool_w_v).then_inc(sems["pw"], 16), "pw",
              writes=["poolw"])
        P.op("dve", lambda e: e.memset(ones[:], 1.0), writes=["ones"])

        def dump(idx, src, reads):
            if not debug:
                return
            P.dma("pool", lambda e: e.dma_start(out=dbg_d[idx], in_=src).then_inc(sems[f"d{idx}"], 16), f"d{idx}",
                  reads=reads)

        def emit_tile(t, prev_tail):
            c0 = HALO + t * TT
            last = (t == NTILE - 1)

            def load_xres():
                for q in range(4):
                    P.dma("sp", lambda e, q=q: e.dma_start(out=xres[:, 4 * q:4 * q + 4, :],
                                                           in_=xh_v[:, 4 * q:4 * q + 4, c0:c0 + TT]).then_inc(sems[f"x{q}"], 16),
                          f"x{q}", writes=[("x", k) for k in range(4 * q, 4 * q + 4)])
            if t == 0:
                load_xres()
                P.dma("sp", lambda e: e.dma_start(out=xhalo[:], in_=xh_v[:, :, 0:HALO]).then_inc(sems["xhl"], 16),
                      "xhl", writes=["xhalo"])
                sb0 = next_bank(hold=True)
                for kc in range(KC):
                    q = next_sq()
                    act_square(sqt[q.i][:], xres[:, kc, :], [("x", kc)], [q])
                    stat_mm(sb0, q, kc == 0, kc == KC - 1)
                r0 = rsqrt_from_bank(sb0, D)
                release(sb0)
                dump(0, sct[r0.i][:], [r0])
                for kc in range(KC):
                    P.op("dve", lambda e, kc=kc: e.scalar_tensor_tensor(out=xgp(kc), in0=xres[:, kc, :], scalar=cst(C_G0 + kc),
                                                                        in1=sct[r0.i][:], op0=ALU.mult, op1=ALU.mult),
                         reads=[("x", kc), r0, "consts"], writes=[("XP", kc)])
                dump(1, xgp(0), [("XP", 0)])
                dump(15, xgp(15), [("XP", 15)])
                act_square(sqh[:], xhalo[:], ["xhalo"], ["sqh"])
                sbh = next_bank()
                mm_group(sbh, lambda k: ones[:], lambda k: sqh[:, k, :], KC, reads=["sqh", "ones"], ncols=HALO)
                rh = rsqrt_from_bank(sbh, D, ncols=HALO)
                for kc in range(KC):
                    P.op("dve", lambda e, kc=kc: e.scalar_tensor_tensor(out=xgh[:, kc, :], in0=xhalo[:, kc, :],
                                                                        scalar=cst(C_G0 + kc), in1=sct[rh.i][:, 0:HALO],
                                                                        op0=ALU.mult, op1=ALU.mult),
                         reads=["xhalo", rh, "consts"], writes=["xgh"])

            def make_prefetch(tn):
                cn = HALO + tn * TT
                st = {"bank": None, "r": None, "n": 0}

                def load(kc):
                    i = st["n"] % 2
                    st["n"] += 1
                    P.dma("sp", lambda e: e.dma_start(out=xst[i][:], in_=xh_v[:, kc, cn:cn + TT]).then_inc(sems[f"xs{i}"], 16),
                          f"xs{i}", writes=[("xst", i)])
                    return i

                steps = []
                pending = {}

                def s_load(n):
                    def f():
                        pending[n] = load(n % KC)
                    return f

                def s_comp(n):
                    def f():
                        i = pending.pop(n)
                        kc = n % KC
                        if n < KC:
                            if n == 0:
                                st["bank"] = next_bank(hold=True)
                            q = next_sq()
                            act_square(sqt[q.i][:], xst[i][:], [("xst", i)], [q])
                            stat_mm(st["bank"], q, kc == 0, kc == KC - 1)
                            if kc == KC - 1:
                                st["r"] = rsqrt_from_bank(st["bank"], D)
                                release(st["bank"])
                        else:
                            r = st["r"]
                            P.op("dve", lambda e: e.scalar_tensor_tensor(out=xgp(kc), in0=xst[i][:], scalar=cst(C_G0 + kc),
                                                                         in1=sct[r.i][:], op0=ALU.mult, op1=ALU.mult),
                                 reads=[("xst", i), r, "consts"], writes=[("XP", kc)])
                    return f

                steps.append(s_load(0))
                for n in range(2 * KC):
                    if n + 1 < 2 * KC:
                        steps.append(s_load(n + 1))
                    steps.append(s_comp(n))
                return steps

            U = [dict() for _ in range(8)]
            xg_reads = [("XP", k) for k in range(KC)]

            def M(j):
                s, wv = use_stage("in")
                u = U[j]
                if t == 0 and j == 0:
                    gl = []
                    for name, col in (("C", 1), ("u", 2), ("v", 3), ("B", 0)):
                        b = next_bank()
                        u[name] = b
                        gl.append((b, (lambda k, col=col: wv[:, k, col * 128:(col + 1) * 128]), (lambda k: xgp(k))))
                    mm_multi(gl, range(KC), lambda k: [("XP", k)], [("slot", s)], KC, 0)
                else:
                    for name, col in (("C", 1), ("u", 2), ("v", 3), ("B", 0)):
                        b = next_bank()
                        u[name] = b
                        mm_group(b, lambda k, col=col: wv[:, k, col * 128:(col + 1) * 128], lambda k: xgp(k), KC,
                                 reads=xg_reads + [("slot", s)])
                if t == 0:
                    b = next_bank()
                    u["halo"] = b
                    for hi, col in enumerate((1, 2, 3)):
                        mm_group(b, lambda k, col=col: wv[:, k, col * 128:(col + 1) * 128], lambda k: xgh[:, k, :], KC,
                                 reads=["xgh", ("slot", s)], c_lo=16 * hi, ncols=HALO)

            def E(j):
                u = U[j]
                HQ = {"h": [], "q": []}
                cur_list = ["h"]

                def OP(eng, fn, reads, writes):
                    HQ[cur_list[0]].append((eng, fn, reads, writes))

                def ACOPY(out, in_, reads, writes):
                    OP("act", lambda e: e.activation(out=out, in_=in_, func=AF.Copy), reads, writes)
                hC, hp, hcu = K5(0), K5(1), K8(0)
                hy = K5(2 + j % 2)
                Csb, p, cu, y = W5(0), W5(1), W8(0), W5(2 + j % 2)
                u["y"] = 2 + j % 2
                bC, bu, bB, bv = u["C"], u["u"], u["B"], u["v"]
                ACOPY(Csb, banks[bC.i][:], [bC], hC)
                OP("dve", lambda e: e.tensor_tensor(out=cu[:, 2:2 + TT], in0=Csb, in1=banks[bu.i][:], op=ALU.mult),
                     reads=hC + [bu], writes=hcu)
                if t == 0:
                    bh = u["halo"]
                    ACOPY(small[:, 0:2], banks[bh.i][:, 14:16], [bh], ["small"])
                    OP("dve", lambda e: e.tensor_tensor(out=cu[:, 0:2], in0=small[:, 0:2], in1=banks[bh.i][:, 30:32],
                                                          op=ALU.mult),
                         reads=["small", bh], writes=hcu)
                else:
                    ACOPY(cu[:, 0:2], carry_cu[:, j, :], [("ccu", j)], hcu)
                if not last:
                    ACOPY(carry_cu[:, j, :], cu[:, TT:TT + 2], hcu, [("ccu", j)])
                OP("dve", lambda e: e.tensor_scalar(out=p, in0=cu[:, 2:2 + TT], scalar1=cst(C_CONV + 2 * 8 + j),
                                                      scalar2=None, op0=ALU.mult),
                     reads=hcu + ["consts"], writes=hp)
                OP("dve", lambda e: e.scalar_tensor_tensor(out=p, in0=cu[:, 1:1 + TT], scalar=cst(C_CONV + 1 * 8 + j),
                                                             in1=p, op0=ALU.mult, op1=ALU.add),
                     reads=hcu + hp + ["consts"], writes=hp)
                OP("dve", lambda e: e.scalar_tensor_tensor(out=p, in0=cu[:, 0:TT], scalar=cst(C_CONV + 0 * 8 + j),
                                                             in1=p, op0=ALU.mult, op1=ALU.add),
                     reads=hcu + hp + ["consts"], writes=hp)
                OP("dve", lambda e: e.tensor_tensor(out=y, in0=p, in1=banks[bB.i][:], op=ALU.mult),
                     reads=hp + [bB], writes=hy)
                q = next_sq()
                u["ysq"] = q
                OP("act", lambda e: e.activation(out=sqt[q.i][:], in_=y, func=AF.Square), hy, [q])
                cur_list[0] = "q"
                hv, hA, hB = K8(1), K8(2), K8(3)
                vb, sA, sB = W8(1), W8(2), W8(3)
                g = j // 2
                wwin = 2 << g
                ACOPY(vb[:, 16:16 + TT], banks[bv.i][:], [bv], hv)
                if t == 0:
                    ACOPY(vb[:, 0:16], banks[bh.i][:, 32:48], [bh], hv)
                else:
                    ACOPY(vb[:, 0:16], carry_v[:, j, :], [("cv", j)], hv)
                if not last:
                    ACOPY(carry_v[:, j, :], vb[:, TT:TT + 16], hv, [("cv", j)])
                OP("dve", lambda e: e.tensor_tensor(out=sA[:, 1:528], in0=vb[:, 1:528], in1=vb[:, 0:527], op=ALU.add),
                     reads=hv, writes=hA)
                cur, hcur = sA, hA
                if g >= 1:
                    OP("dve", lambda e: e.tensor_tensor(out=sB[:, 3:528], in0=sA[:, 3:528], in1=sA[:, 1:526], op=ALU.add),
                         reads=hA, writes=hB)
                    cur, hcur = sB, hB
                if g >= 2:
                    OP("dve", lambda e: e.tensor_tensor(out=sA[:, 7:528], in0=sB[:, 7:528], in1=sB[:, 3:524], op=ALU.add),
                         reads=hB, writes=hA)
                    cur, hcur = sA, hA
                if g >= 3:
                    OP("dve", lambda e: e.tensor_tensor(out=sB[:, 15:528], in0=sA[:, 15:528], in1=sA[:, 7:520], op=ALU.add),
                         reads=hA, writes=hB)
                    cur, hcur = sB, hB
                pj = j % 4
                OP("dve", lambda e: e.scalar_tensor_tensor(out=PL(pj), in0=cur[:, 16:16 + TT], scalar=1.0 / wwin,
                                                             in1=vb[:, 16:16 + TT], op0=ALU.mult, op1=ALU.subtract),
                     reads=hcur + hv, writes=KPL(pj))
                if t == 0:
                    OP("dve", lambda e: e.tensor_tensor(out=small[:, 16:32], in0=cur[:, 16:32],
                                                          in1=consts[:, C_INV + g * 16:C_INV + (g + 1) * 16], op=ALU.mult),
                         reads=hcur + ["consts"], writes=["small2"])
                    OP("dve", lambda e: e.tensor_tensor(out=PL(pj)[:, 0:16], in0=small[:, 16:32], in1=vb[:, 16:32],
                                                          op=ALU.subtract),
                         reads=["small2"] + hv, writes=KPL(pj))
                hl, ql = HQ["h"], HQ["q"]
                for n in range(max(len(hl), len(ql))):
                    if n < len(hl):
                        P.op(hl[n][0], hl[n][1], reads=hl[n][2], writes=hl[n][3])
                    if n < len(ql):
                        P.op(ql[n][0], ql[n][1], reads=ql[n][2], writes=ql[n][3])
                if t == 0 and j == 0:
                    dump(2, cu[:, 2:2 + TT], hcu)
                    dump(14, Csb, hC)
                    dump(3, y, hy)
                    dump(5, PL(pj), KPL(pj))

            def S(j):
                u = U[j]
                b = next_bank()
                u["hst"] = b
                stat_mm(b, u["ysq"], True, True)
                if j % 2 == 1:
                    g = j // 2
                    u["yp"] = []
                    for e2 in range(2):
                        b2 = next_bank()
                        u["yp"].append(b2)
                        mm_group(b2, lambda k, e2=e2: poolw[:, g, k, e2 * 128:(e2 + 1) * 128],
                                 lambda k: PL((2 * g + k) % 4), 2,
                                 reads=KPL((2 * g) % 4) + KPL((2 * g + 1) % 4) + ["poolw"])

            def F(j):
                u = U[j]
                r = rsqrt_from_bank(u["hst"], 128)
                iy = u["y"]
                P.op("dve", lambda e: e.tensor_tensor(out=mixed(j), in0=W5(iy), in1=sct[r.i][:], op=ALU.mult),
                     reads=K5(iy) + [r], writes=[("A", 16 + j)])
                if t == 0 and j == 0:
                    dump(4, mixed(0), [("A", 16)])
                if j % 2 == 1:
                    u["yps"] = []
                    u["ypsq"] = []
                    for e2 in range(2):
                        iw = 4 + e2
                        q = next_sq()
                        b2 = u["yp"][e2]
                        u["yps"].append(iw)
                        u["ypsq"].append(q)
                        act_copy(W5(iw), banks[b2.i][:], [b2], K5(iw))
                        act_square(sqt[q.i][:], banks[b2.i][:], [b2], [q])

            def S2(j):
                if j % 2 != 1:
                    return
                u = U[j]
                b = next_bank()
                u["pst"] = b
                for e2 in range(2):
                    stat_mm(b, u["ypsq"][e2], e2 == 0, e2 == 1)

            def F2(j):
                if j % 2 != 1:
                    return
                u = U[j]
                g = j // 2
                r = rsqrt_from_bank(u["pst"], 256)
                for e2 in range(2):
                    iw = u["yps"][e2]
                    c = 2 * g + e2
                    P.op("dve", lambda e, iw=iw, c=c: e.scalar_tensor_tensor(out=mixed(8 + c), in0=W5(iw),
                                                                             scalar=cst(C_PSC + c), in1=sct[r.i][:],
                                                                             op0=ALU.mult, op1=ALU.mult),
                         reads=K5(iw) + [r, "consts"], writes=[("A", 16 + 8 + c)])
                if t == 0 and j == 1:
                    dump(6, mixed(8), [("A", 24)])

            OUT_EARLY = [0, 1, 2, 3, 4, 5, 6, 8, 9, 10, 11, 12, 13]
            OUT_LATE = [7, 14, 15]
            op0 = {}

            def out_stage0_early():
                s0, wv0 = use_stage("out")
                op0["s"], op0["wv"] = s0, wv0
                op0["banks"] = [next_bank(hold=True) for _ in range(4)]
                op0["gl"] = [(op0["banks"][i], (lambda k, i=i: wv0[:, k, i * 128:(i + 1) * 128]), (lambda k: mixed(k)))
                             for i in range(4)]
                mm_multi(op0["gl"], OUT_EARLY, lambda k: [("A", 16 + k)], [("slot", s0)], KC, 0)

            for step in range(8 + 2):
                if step < 8:
                    M(step)
                if step == 8:
                    out_stage0_early()
                if 0 <= step - 1 < 8:
                    S(step - 1)
                if 0 <= step - 2 < 8:
                    S2(step - 2)
                if step < 8:
                    E(step)
                if step < 8 and prev_tail:
                    for st_ in prev_tail[2 * step:2 * step + 2]:
                        st_()
                    if step == 7:
                        load_xres()
                if 0 <= step - 1 < 8:
                    F(step - 1)
                if 0 <= step - 2 < 8:
                    F2(step - 2)

            mixed_reads = [("A", 16 + k) for k in range(KC)]
            sb1 = next_bank(hold=True)
            pend = None

            def evac_mo(i, b):
                q = next_sq()
                act_square(sqt[q.i][:], banks[b.i][:], [b], [q])
                stat_mm(sb1, q, i == 0, i == KC - 1)
                act_copy(mov(i), banks[b.i][:], [b], [("B", 2 * i), ("B", 2 * i + 1)])

            mm_multi(op0["gl"], OUT_LATE, lambda k: [("A", 16 + k)], [("slot", op0["s"])], KC, len(OUT_EARLY))
            for b in op0["banks"]:
                release(b)
            for i in range(3):
                evac_mo(i, op0["banks"][i])
            pend = (3, op0["banks"][3])
            for i in range(4, KC):
                if i % 4 == 0:
                    s, wv = use_stage("out")
                b = next_bank()
                mm_group(b, lambda k, i=i, wv=wv: wv[:, k, (i % 4) * 128:(i % 4 + 1) * 128], lambda k: mixed(k), KC,
                         reads=mixed_reads + [("slot", s)])
                if pend is not None:
                    evac_mo(*pend)
                pend = (i, b)
            evac_mo(*pend)
            rs1 = rsqrt_from_bank(sb1, D)
            release(sb1)
            sb2 = next_bank(hold=True)
            s_g0, wv_g0 = use_stage("gu")
            gb = [next_bank() for _ in range(4)]
            gcols = [0, 256, 128, 384]
            gl0 = [(gb[n], (lambda k, c=gcols[n]: wv_g0[:, k, c:c + 128]), (lambda k: xg(k))) for n in range(4)]
            BURST_AFTER = {5: range(0, 6), 9: range(6, 10), 12: range(10, 13), 14: range(13, 15), 15: range(15, 16)}
            def x1_stt(i):
                P.op("dve", lambda e, i=i: e.scalar_tensor_tensor(out=mov(i), in0=mov(i), scalar=cst(C_G1 + i),
                                                                  in1=sct[rs1.i][:], op0=ALU.mult, op1=ALU.mult),
                     reads=[("B", 2 * i), ("B", 2 * i + 1), rs1, "consts"], writes=[("B", 2 * i), ("B", 2 * i + 1)])

            x1_stt(0)
            for i in range(KC):
                if i + 1 < KC:
                    x1_stt(i + 1)
                P.op("dve", lambda e, i=i: e.tensor_tensor(out=xres[:, i, :], in0=mov(i), in1=xres[:, i, :], op=ALU.add),
                     reads=[("B", 2 * i), ("B", 2 * i + 1), ("x", i)], writes=[("x", i)])
                q = next_sq()
                act_square(sqt[q.i][:], xres[:, i, :], [("x", i)], [q])
                stat_mm(sb2, q, i == 0, i == KC - 1)
                P.op("act", lambda e, i=i: e.activation(out=xg(i), in_=xres[:, i, :], func=AF.Copy, scale=cst(C_G2 + i)),
                     reads=[("x", i), "consts"], writes=[("A", i)])
                if i in BURST_AFTER:
                    ks = list(BURST_AFTER[i])
                    mm_multi(gl0, ks, lambda k: [("A", k)], [("slot", s_g0)], KC, ks[0])
            r2 = rsqrt_from_bank(sb2, D)
            release(sb2)
            if t == 0:
                dump(7, xres[:, 0, :], [("x", 0)])
                dump(11, sct[rs1.i][:], [rs1])
                dump(12, sct[r2.i][:], [r2])

            hf_reads = [("A", k) for k in range(KC)]
            pend = None
            sgi = [0]

            pf_steps = make_prefetch(t + 1) if not last else []
            pf_pos = [0]

            def pf_advance(n):
                for _ in range(n):
                    if pf_pos[0] < len(pf_steps):
                        pf_steps[pf_pos[0]]()
                        pf_pos[0] += 1

            def evac_gu(f, bg, bu):
                hg = P.alloc("w3", 2 * (sgi[0] % 2))
                hu = P.alloc("w3", 2 * (sgi[0] % 2) + 1)
                sgi[0] += 1
                tg, tu = w3[hg.i], w3[hu.i]
                P.op("dve", lambda e: e.tensor_tensor(out=tg[:], in0=banks[bg.i][:], in1=sct[r2.i][:], op=ALU.mult),
                     reads=[bg, r2], writes=[hg])
                P.op("dve", lambda e: e.tensor_tensor(out=tu[:], in0=banks[bu.i][:], in1=sct[r2.i][:], op=ALU.mult),
                     reads=[bu, r2], writes=[hu])
                P.op("act", lambda e: e.activation(out=tg[:], in_=tg[:], func=AF.Silu),
                     reads=[hg], writes=[hg])
                P.op("dve", lambda e: e.tensor_tensor(out=regB[:, f, :], in0=tg[:], in1=tu[:], op=ALU.mult),
                     reads=[hg, hu], writes=[("B", f)])

            evac_gu(0, gb[0], gb[1])
            pend = (1, gb[2], gb[3])
            pf_advance(4)
            for f in range(2, FC):
                if f % 2 == 0:
                    s, wv = use_stage("gu")
                e2 = f % 2
                bg = next_bank()
                mm_group(bg, lambda k, wv=wv, e2=e2: wv[:, k, e2 * 128:(e2 + 1) * 128], lambda k: xg(k), KC,
                         reads=hf_reads + [("slot", s)])
                bu = next_bank()
                mm_group(bu, lambda k, wv=wv, e2=e2: wv[:, k, 256 + e2 * 128:256 + (e2 + 1) * 128], lambda k: xg(k), KC,
                         reads=hf_reads + [("slot", s)])
                if pend is not None:
                    evac_gu(*pend)
                pend = (f, bg, bu)
                pf_advance(2)
            evac_gu(*pend)
            pf_advance(len(pf_steps))
            if t == 0:
                dump(9, regB[:, 0, :], [("B", 0)])

            sb3 = next_bank(hold=True)
            pend = []

            def evac_ff(i, b):
                q = next_sq()
                act_square(sqt[q.i][:], banks[b.i][:], [b], [q])
                stat_mm(sb3, q, i == 0, i == KC - 1)
                act_copy(ffv(i), banks[b.i][:], [b], [("A", 2 * i), ("A", 2 * i + 1)])

            for c2 in range(8):
                bb = [next_bank(), next_bank()]
                for h in range(2):
                    s, wv = use_stage("dn")
                    for e2 in range(2):
                        mm_group(bb[e2], lambda k, wv=wv, e2=e2: wv[:, k, e2 * 128:(e2 + 1) * 128],
                                 lambda k, h=h: regB[:, h * 22 + k, :], 22,
                                 reads=[("B", h * 22 + k) for k in range(22)] + [("slot", s)],
                                 start=(h == 0), stop=(h == 1))
                for (i, b) in pend:
                    evac_ff(i, b)
                pend = [(2 * c2, bb[0]), (2 * c2 + 1, bb[1])]
            for (i, b) in pend:
                evac_ff(i, b)
            rsqrt_from_bank(sb3, D, fixed=sc_rs3)
            release(sb3)
            if t == 0:
                dump(10, ffv(0), [("A", 0), ("A", 1)])
                dump(13, sc_rs3[:], ["rs3"])

            def tail_scale(i):
                P.op("dve", lambda e: e.scalar_tensor_tensor(out=ffv(i), in0=ffv(i), scalar=cst(C_G3 + i),
                                                             in1=sc_rs3[:], op0=ALU.mult, op1=ALU.mult),
                     reads=[("A", 2 * i), ("A", 2 * i + 1), "rs3", "consts"], writes=[("A", 2 * i), ("A", 2 * i + 1)])

            def tail_add(i, nstore=4):
                P.op("dve", lambda e: e.tensor_tensor(out=xres[:, i, :], in0=ffv(i), in1=xres[:, i, :], op=ALU.add),
                     reads=[("A", 2 * i), ("A", 2 * i + 1), ("x", i)], writes=[("x", i)])
                if i % nstore == nstore - 1:
                    lo = i - nstore + 1
                    q = i // 4
                    P.dma("sp", lambda e: e.dma_start(out=yT_v[:, lo:i + 1, t * TT:(t + 1) * TT],
                                                      in_=xres[:, lo:i + 1, :]).then_inc(sems[f"o{q}"], 16),
                          f"o{q}", reads=[("x", k) for k in range(lo, i + 1)])

            if last:
                tail_scale(0)
                for i in range(KC):
                    if i + 1 < KC:
                        tail_scale(i + 1)
                    tail_add(i, nstore=2)
                return []

            order = [8, 12, 9, 13, 10, 14, 11, 15, 0, 1, 2, 3, 4, 5, 6, 7]

            def pair_step(a, b):
                def f():
                    tail_scale(a)
                    tail_scale(b)
                    tail_add(a)
                    tail_add(b)
                return f
            steps = []
            for n in range(0, KC, 2):
                steps.append(pair_step(order[n], order[n + 1]))
                steps.append(lambda: None)
            return steps

        tail = []
        for t in range(NTILE):
            tail = emit_tile(t, tail)
        P.wait_all("sp", [(f"o{q}", P.dma_cnt[f"o{q}"]) for q in range(4)])

        @block.sync
        def _(e):
            P.replay("sp", e, sems)

        @block.gpsimd
        def _(e):
            P.replay("pool", e, sems)

        @block.tensor
        def _(e):
            P.replay("pe", e, sems)

        @block.scalar
        def _(e):
            P.replay("act", e, sems)

        @block.vector
        def _(e):
            P.replay("dve", e, sems)

    return nc


_PROGRAM = None


def _host_layout(x, ln_mix_pre, w_in, conv_w, pool_w, pool_scale, w_out, ln_mix_post,
                 ln_ffn_pre, w_gate, w_up, w_down, ln_ffn_post):
    f32 = np.float32
    x = np.asarray(x, f32)
    w_in = np.asarray(w_in, f32)[0]
    w_in_p = w_in.reshape(D, 4, 8, 128).transpose(0, 2, 1, 3).reshape(D, 4096)
    wg = np.asarray(w_gate, f32)[0].reshape(D, 22, 256)
    wu = np.asarray(w_up, f32)[0].reshape(D, 22, 256)
    w_gu = np.stack([wg, wu], axis=2).reshape(D, 2 * DFF)

    def stage_major(w, ncol):
        K, N = w.shape
        return np.ascontiguousarray(
            w.reshape(K // 128, 128, N // ncol, ncol).transpose(2, 1, 0, 3).reshape(N // ncol, 128, (K // 128) * ncol))

    wst_in = stage_major(w_in_p, 512)
    wst_out = stage_major(np.asarray(w_out, f32)[0], 512)
    wst_gu = stage_major(w_gu, 512)
    wd = np.asarray(w_down, f32)[0]
    wst_dn = np.ascontiguousarray(
        wd.reshape(2, 22, 128, 8, 256).transpose(3, 0, 2, 1, 4).reshape(16, 128, 22 * 256))
    pool_w_ = np.ascontiguousarray(np.asarray(pool_w, f32)[0])

    def pk(v):
        return np.asarray(v, f32).reshape(-1, 128).T

    base = np.zeros((128, NCONST), f32)
    base[:, C_G0:C_G0 + 16] = pk(ln_mix_pre[0])
    base[:, C_G1:C_G1 + 16] = pk(ln_mix_post[0])
    base[:, C_G2:C_G2 + 16] = pk(ln_ffn_pre[0])
    base[:, C_G3:C_G3 + 16] = pk(ln_ffn_post[0])
    cw = np.asarray(conv_w, f32)[0]
    for j in range(3):
        base[:, C_CONV + j * 8:C_CONV + (j + 1) * 8] = cw[j].reshape(8, 128).T
    base[:, C_PSC:C_PSC + 8] = np.asarray(pool_scale, f32)[0].reshape(8, 128).T

    in_maps = []
    for c in range(N_CORES):
        b, q = divmod(c, 4)
        xh = np.zeros((D, HALO + TOK), f32)
        lo = q * TOK - HALO
        if lo >= 0:
            xh[:, :] = x[b, lo:(q + 1) * TOK, :].T
        else:
            xh[:, HALO:] = x[b, 0:TOK, :].T
        cst = base.copy()
        for g, w in enumerate((2, 4, 8, 16)):
            for i in range(16):
                cnt = min(i + 1, w) if q == 0 else w
                cst[:, C_INV + g * 16 + i] = 1.0 / cnt
        in_maps.append({"xh": xh, "consts": cst, "wst_in": wst_in, "pool_w": pool_w_, "wst_out": wst_out,
                        "wst_gu": wst_gu, "wst_dn": wst_dn})
    return in_maps


def kernel(x, ln_mix_pre, w_in, conv_w, pool_w, pool_scale, w_out, ln_mix_post,
           ln_ffn_pre, w_gate, w_up, w_down, ln_ffn_post):
    global _PROGRAM
    in_maps = _host_layout(x, ln_mix_pre, w_in, conv_w, pool_w, pool_scale, w_out, ln_mix_post,
                           ln_ffn_pre, w_gate, w_up, w_down, ln_ffn_post)
    if _PROGRAM is None:
        _PROGRAM = build_program()
    res = run_bass_kernel_spmd(_PROGRAM, in_maps, core_ids=list(range(N_CORES)))
    out = np.empty((2, 4 * TOK, D), np.float32)
    for c in range(N_CORES):
        b, q = divmod(c, 4)
        out[b, q * TOK:(q + 1) * TOK, :] = np.asarray(res.results[c]["yT"], np.float32).T
    return out
```

```python
import numpy as np
import concourse.bass as bass
import concourse.mybir as mybir
from concourse.bass_utils import run_bass_kernel_spmd

F32 = mybir.dt.float32
BF16 = mybir.dt.bfloat16
AF = mybir.ActivationFunctionType
ALU = mybir.AluOpType

N_CORES = 8
D = 2048
KC = 16
TT = 512
NTILE = 4
TOK = 2048
HALO = 16
DFF = 5632
FC = 44
EPS = 1e-6
NCONST = 160
C_G0, C_G1, C_G2, C_G3, C_CONV, C_PSC, C_INV = 0, 16, 32, 48, 64, 88, 96
NSC = 4
NSQ = 4
SAME_ENGINE_SYNC = True

ENGS = ("pe", "act", "dve", "pool", "sp")


class R:
    __slots__ = ("kind", "i", "gen")

    def __init__(self, kind, i, gen):
        self.kind, self.i, self.gen = kind, i, gen


class Prog:
    def __init__(self):
        self.gen = {}
        self.ops = {e: [] for e in ENGS}
        self.cnt = {e: 0 for e in ENGS}
        self.res = {}
        self.waited = {e: {} for e in ENGS}
        self.dma_cnt = {}

    def alloc(self, kind, i):
        g = self.gen.get((kind, i), 0) + 1
        self.gen[(kind, i)] = g
        return R(kind, i, g)

    def _norm(self, lst):
        out = []
        for r in lst:
            if isinstance(r, R):
                assert self.gen[(r.kind, r.i)] == r.gen, ("stale handle", r.kind, r.i, r.gen, self.gen[(r.kind, r.i)])
                out.append((r.kind, r.i))
            else:
                out.append(r)
        return out

    def _collect(self, eng, reads, writes):
        need = {}

        def add(tok):
            if tok is None:
                return
            k, v = tok
            if k == eng and (eng == "pe" or not SAME_ENGINE_SYNC):
                return
            if need.get(k, 0) < v:
                need[k] = v

        for r in reads:
            st = self.res.get(r)
            if st is not None:
                add(st["w"])
        for w in writes:
            st = self.res.get(w)
            if st is not None:
                add(st["w"])
                for tk in st["r"]:
                    add(tk)
        waits = []
        wd = self.waited[eng]
        for k, v in need.items():
            if wd.get(k, 0) >= v:
                continue
            wd[k] = v
            waits.append((k, v))
        return waits

    def _update(self, tok, reads, writes):
        for r in reads:
            st = self.res.setdefault(r, {"w": None, "r": []})
            st["r"].append(tok)
        for w in writes:
            self.res[w] = {"w": tok, "r": []}

    def op(self, eng, fn, reads=(), writes=()):
        reads, writes = self._norm(reads), self._norm(writes)
        waits = self._collect(eng, reads, writes)
        self.cnt[eng] += 1
        tok = (eng, self.cnt[eng])
        self.ops[eng].append((waits, fn, True))
        self._update(tok, reads, writes)
        return tok

    def dma(self, eng, fn, semkey, reads=(), writes=(), n=1):
        reads, writes = self._norm(reads), self._norm(writes)
        waits = self._collect(eng, reads, writes)
        self.dma_cnt[semkey] = self.dma_cnt.get(semkey, 0) + 16 * n
        tok = (semkey, self.dma_cnt[semkey])
        self.ops[eng].append((waits, fn, False))
        self._update(tok, reads, writes)
        return tok

    def wait_all(self, eng, toks):
        waits = []
        for k, v in toks:
            if self.waited[eng].get(k, 0) < v:
                self.waited[eng][k] = v
                waits.append((k, v))
        self.ops[eng].append((waits, None, False))

    def replay(self, eng, e, sems):
        for waits, fn, signal in self.ops[eng]:
            for k, v in waits:
                e.wait_ge(sems[k], v)
            if fn is None:
                continue
            inst = fn(e)
            if signal:
                inst.then_inc(sems[eng], 1)


NDBG = 16


def build_program(debug=False):
    nc = bass.Bass("TRN2", target_bir_lowering=False)
    dbg_d = nc.dram_tensor("dbg", [NDBG, 128, TT], F32, kind="ExternalOutput").ap() if debug else None
    xh = nc.dram_tensor("xh", [D, HALO + TOK], F32, kind="ExternalInput").ap()
    consts_d = nc.dram_tensor("consts", [128, NCONST], F32, kind="ExternalInput").ap()
    w_in_d = nc.dram_tensor("wst_in", [8, 128, 8192], F32, kind="ExternalInput").ap()
    pool_w_d = nc.dram_tensor("pool_w", [4, 256, 256], F32, kind="ExternalInput").ap()
    w_out_d = nc.dram_tensor("wst_out", [4, 128, 8192], F32, kind="ExternalInput").ap()
    w_gu_d = nc.dram_tensor("wst_gu", [22, 128, 8192], F32, kind="ExternalInput").ap()
    w_dn_d = nc.dram_tensor("wst_dn", [16, 128, 5632], F32, kind="ExternalInput").ap()
    yT = nc.dram_tensor("yT", [D, TOK], F32, kind="ExternalOutput").ap()

    xh_v = xh.rearrange("(kc p) t -> p kc t", p=128)
    yT_v = yT.rearrange("(kc p) t -> p kc t", p=128)
    pool_w_v = pool_w_d.rearrange("g (kc p) j -> p g kc j", p=128)

    sem_names = [f"d{i}" for i in range(NDBG if debug else 0)] + list(ENGS) + ["s0", "s1", "s2", "x0", "x1", "x2", "x3", "o0", "o1", "o2", "o3", "misc", "pw", "xhl", "xs0", "xs1"]

    from contextlib import ExitStack
    with ExitStack() as es:
        def sb(name, shape, dt):
            return es.enter_context(nc.sbuf_tensor(name, shape, dt))

        xres = sb("xres", [128, KC, TT], F32)
        regA = sb("regA", [128, 32, TT], BF16)
        regB = sb("regB", [128, FC, TT], BF16)
        slots = [sb(f"slot{i}", [128, 8192], BF16) for i in range(3)]
        xgP = sb("xgP", [128, KC, TT], BF16)
        w3 = [sb(f"w3_{i}", [128, TT], F32) for i in range(4)]
        xst = [sb(f"xst{i}", [128, TT], F32) for i in range(2)]
        sct = [sb(f"sc{i}", [128, TT], F32) for i in range(NSC)]
        sqt = [sb(f"sq{i}", [128, TT], BF16) for i in range(NSQ)]
        sc_rs3 = sb("sc_rs3", [128, TT], F32)
        consts = sb("constsb", [128, NCONST], F32)
        poolw = sb("poolw", [128, 4, 2, 256], BF16)
        ones = sb("ones", [128, 128], BF16)
        carry_cu = sb("carry_cu", [128, 8, 2], F32)
        carry_v = sb("carry_v", [128, 8, 16], F32)
        xhalo = sb("xhalo", [128, KC, HALO], F32)
        xgh = sb("xgh", [128, KC, HALO], BF16)
        sqh = sb("sqh", [128, KC, HALO], BF16)
        small = sb("small", [128, 64], F32)
        banks = [es.enter_context(nc.psum_tensor(f"bank{i}", [128, TT], F32)) for i in range(8)]
        sems = {n: es.enter_context(nc.semaphore(n)) for n in sem_names}
        block = es.enter_context(nc.Block())

        P = Prog()
        rr = {"bank": 0, "sc": 0, "sq": 0}
        held = set()

        def next_bank(hold=False):
            for _ in range(9):
                b = rr["bank"]
                rr["bank"] = (b + 1) % 8
                if b not in held:
                    break
            else:
                raise RuntimeError("no free bank")
            if hold:
                held.add(b)
            return P.alloc("ps", b)

        def release(h):
            held.discard(h.i)

        def next_sc():
            i = rr["sc"]
            rr["sc"] = (i + 1) % NSC
            return P.alloc("sc", i)

        def next_sq():
            i = rr["sq"]
            rr["sq"] = (i + 1) % NSQ
            return P.alloc("sq", i)

        def cst(c):
            return consts[:, c:c + 1]

        def xg(k):
            return regA[:, k, :]

        def mixed(k):
            return regA[:, 16 + k, :]

        def ffv(i):
            return regA[:, 2 * i:2 * i + 2, :].bitcast(F32).rearrange("p a b -> p (a b)")

        def mov(i):
            return regB[:, 2 * i:2 * i + 2, :].bitcast(F32).rearrange("p a b -> p (a b)")

        def W5(i):
            return mov(i)

        def K5(i):
            return [("B", 2 * i), ("B", 2 * i + 1)]

        def W8(i):
            return regB[:, 32 + 3 * i:35 + 3 * i, :].bitcast(F32).rearrange("p a b -> p (a b)")[:, 0:528]

        def K8(i):
            return [("B", 32 + 3 * i + d) for d in range(3)]

        def PL(pj):
            return regB[:, 12 + pj, :]

        def KPL(pj):
            return [("B", 12 + pj)]

        def xgp(k):
            return xgP[:, k, :]

        stages = []
        for t in range(NTILE):
            for j in range(8):
                stages.append(("in", j))
            for j in range(4):
                stages.append(("out", j))
            for j in range(22):
                stages.append(("gu", j))
            for c2 in range(8):
                for h in range(2):
                    stages.append(("dn", c2, h))
        st_state = {"next_load": 0, "next_use": 0}

        def emit_load(n):
            st = stages[n]
            s = n % 3
            slot = slots[s]
            if st[0] == "dn":
                dst = slot[:, 0:5632].rearrange("p (a b) -> p a b", b=1408)
                src = w_dn_d[st[1] * 2 + st[2]].rearrange("p (a b) -> p a b", b=1408)
            else:
                srcd = {"in": w_in_d, "out": w_out_d, "gu": w_gu_d}[st[0]]
                dst = slot[:, 0:8192].rearrange("p (a b) -> p a b", b=2048)
                src = srcd[st[1]].rearrange("p (a b) -> p a b", b=2048)
            sk = f"s{s}"
            extra = []
            if n == 0:
                extra = [("x", k) for k in range(KC)]
            elif n in (1, 2):
                extra = [("slot", n - 1)]
            P.dma("pool", lambda e, dst=dst, src=src, sk=sk: e.dma_start(out=dst, in_=src).then_inc(sems[sk], 16),
                  sk, reads=extra, writes=[("slot", s)])

        def use_stage(kind):
            n = st_state["next_use"]
            assert stages[n][0] == kind, (stages[n], kind)
            st_state["next_use"] = n + 1
            while st_state["next_load"] < min(len(stages), n + 3):
                emit_load(st_state["next_load"])
                st_state["next_load"] += 1
            s = n % 3
            if kind == "dn":
                view = slots[s][:, 0:5632].rearrange("p (k n) -> p k n", n=256)
            else:
                view = slots[s][:, 0:8192].rearrange("p (k n) -> p k n", n=512)
            return s, view

        def mm_group(bank, lhs_fn, rhs_fn, nk, reads, c_lo=0, ncols=TT, start=True, stop=True, korder=None,
                     per_k_reads=None):
            ks = list(range(nk)) if korder is None else list(korder)
            if per_k_reads is not None:
                for n, k in enumerate(ks):
                    P.op("pe", lambda e, n=n, k=k: e.matmul(banks[bank.i][:, c_lo:c_lo + ncols], lhsT=lhs_fn(k), rhs=rhs_fn(k),
                                                           start=(start and n == 0), stop=(stop and n == nk - 1)),
                         reads=list(per_k_reads(k)) + list(reads), writes=[bank])
                return

            def fn(e):
                inst = None
                for n, k in enumerate(ks):
                    inst = e.matmul(banks[bank.i][:, c_lo:c_lo + ncols], lhsT=lhs_fn(k), rhs=rhs_fn(k),
                                    start=(start and n == 0), stop=(stop and n == nk - 1))
                return inst
            P.op("pe", fn, reads=reads, writes=[bank])

        def mm_multi(glist, ks, kreads, common, nk_total, done_before):
            for n, k in enumerate(ks):
                pos = done_before + n
                for (bank, lhs_fn, rhs_fn) in glist:
                    P.op("pe", lambda e, bank=bank, lhs_fn=lhs_fn, rhs_fn=rhs_fn, k=k, pos=pos: e.matmul(
                        banks[bank.i][:], lhsT=lhs_fn(k), rhs=rhs_fn(k), start=(pos == 0), stop=(pos == nk_total - 1)),
                         reads=list(kreads(k)) + list(common), writes=[bank])

        def rsqrt_from_bank(bank, n, ncols=TT, fixed=None):
            l = next_sc()
            P.op("act", lambda e: e.activation(out=sct[l.i][:, 0:ncols], in_=banks[bank.i][:, 0:ncols], func=AF.Ln,
                                               scale=1.0 / n, bias=EPS),
                 reads=[bank], writes=[l])
            if fixed is not None:
                P.op("act", lambda e: e.activation(out=fixed[:, 0:ncols], in_=sct[l.i][:, 0:ncols], func=AF.Exp, scale=-0.5),
                     reads=[l], writes=["rs3"])
                return None
            r = next_sc()
            P.op("act", lambda e: e.activation(out=sct[r.i][:, 0:ncols], in_=sct[l.i][:, 0:ncols], func=AF.Exp, scale=-0.5),
                 reads=[l], writes=[r])
            return r

        def act_copy(out, in_, reads, writes):
            P.op("act", lambda e: e.activation(out=out, in_=in_, func=AF.Copy), reads=reads, writes=writes)

        def act_square(out, in_, reads, writes):
            P.op("act", lambda e: e.activation(out=out, in_=in_, func=AF.Square), reads=reads, writes=writes)

        def stat_mm(bank, q, first, last_):
            P.op("pe", lambda e: e.matmul(banks[bank.i][:], lhsT=ones[:], rhs=sqt[q.i][:], start=first, stop=last_),
                 reads=[q, "ones"], writes=[bank])

        P.dma("sp", lambda e: e.dma_start(out=consts[:], in_=consts_d).then_inc(sems["misc"], 16), "misc",
              writes=["consts"])
        P.dma("pool", lambda e: e.dma_start(out=poolw[:], in_=pool_w_v).then_inc(sems["pw"], 16), "pw",
              writes=["poolw"])
        P.op("dve", lambda e: e.memset(ones[:], 1.0), writes=["ones"])

        def dump(idx, src, reads):
            if not debug:
                return
            P.dma("pool", lambda e: e.dma_start(out=dbg_d[idx], in_=src).then_inc(sems[f"d{idx}"], 16), f"d{idx}",
                  reads=reads)

        def emit_tile(t, prev_tail):
            c0 = HALO + t * TT
            last = (t == NTILE - 1)

            def load_xres():
                for q in range(4):
                    P.dma("sp", lambda e, q=q: e.dma_start(out=xres[:, 4 * q:4 * q + 4, :],
                                                           in_=xh_v[:, 4 * q:4 * q + 4, c0:c0 + TT]).then_inc(sems[f"x{q}"], 16),
                          f"x{q}", writes=[("x", k) for k in range(4 * q, 4 * q + 4)])
            if t == 0:
                load_xres()
                P.dma("sp", lambda e: e.dma_start(out=xhalo[:], in_=xh_v[:, :, 0:HALO]).then_inc(sems["xhl"], 16),
                      "xhl", writes=["xhalo"])
                sb0 = next_bank(hold=True)
                for kc in range(KC):
                    q = next_sq()
                    act_square(sqt[q.i][:], xres[:, kc, :], [("x", kc)], [q])
                    stat_mm(sb0, q, kc == 0, kc == KC - 1)
                r0 = rsqrt_from_bank(sb0, D)
                release(sb0)
                dump(0, sct[r0.i][:], [r0])
                for kc in range(KC):
                    P.op("dve", lambda e, kc=kc: e.scalar_tensor_tensor(out=xgp(kc), in0=xres[:, kc, :], scalar=cst(C_G0 + kc),
                                                                        in1=sct[r0.i][:], op0=ALU.mult, op1=ALU.mult),
                         reads=[("x", kc), r0, "consts"], writes=[("XP", kc)])
                dump(1, xgp(0), [("XP", 0)])
                dump(15, xgp(15), [("XP", 15)])
                act_square(sqh[:], xhalo[:], ["xhalo"], ["sqh"])
                sbh = next_bank()
                mm_group(sbh, lambda k: ones[:], lambda k: sqh[:, k, :], KC, reads=["sqh", "ones"], ncols=HALO)
                rh = rsqrt_from_bank(sbh, D, ncols=HALO)
                for kc in range(KC):
                    P.op("dve", lambda e, kc=kc: e.scalar_tensor_tensor(out=xgh[:, kc, :], in0=xhalo[:, kc, :],
                                                                        scalar=cst(C_G0 + kc), in1=sct[rh.i][:, 0:HALO],
                                                                        op0=ALU.mult, op1=ALU.mult),
                         reads=["xhalo", rh, "consts"], writes=["xgh"])

            def make_prefetch(tn):
                cn = HALO + tn * TT
                st = {"bank": None, "r": None, "n": 0}

                def load(kc):
                    i = st["n"] % 2
                    st["n"] += 1
                    P.dma("sp", lambda e: e.dma_start(out=xst[i][:], in_=xh_v[:, kc, cn:cn + TT]).then_inc(sems[f"xs{i}"], 16),
                          f"xs{i}", writes=[("xst", i)])
                    return i

                steps = []
                pending = {}

                def s_load(n):
                    def f():
                        pending[n] = load(n % KC)
                    return f

                def s_comp(n):
                    def f():
                        i = pending.pop(n)
                        kc = n % KC
                        if n < KC:
                            if n == 0:
                                st["bank"] = next_bank(hold=True)
                            q = next_sq()
                            act_square(sqt[q.i][:], xst[i][:], [("xst", i)], [q])
                            stat_mm(st["bank"], q, kc == 0, kc == KC - 1)
                            if kc == KC - 1:
                                st["r"] = rsqrt_from_bank(st["bank"], D)
                                release(st["bank"])
                        else:
                            r = st["r"]
                            P.op("dve", lambda e: e.scalar_tensor_tensor(out=xgp(kc), in0=xst[i][:], scalar=cst(C_G0 + kc),
                                                                         in1=sct[r.i][:], op0=ALU.mult, op1=ALU.mult),
                                 reads=[("xst", i), r, "consts"], writes=[("XP", kc)])
                    return f

                steps.append(s_load(0))
                for n in range(2 * KC):
                    if n + 1 < 2 * KC:
                        steps.append(s_load(n + 1))
                    steps.append(s_comp(n))
                return steps

            U = [dict() for _ in range(8)]
            xg_reads = [("XP", k) for k in range(KC)]

            def M(j):
                s, wv = use_stage("in")
                u = U[j]
                if t == 0 and j == 0:
                    gl = []
                    for name, col in (("C", 1), ("u", 2), ("v", 3), ("B", 0)):
                        b = next_bank()
                        u[name] = b
                        gl.append((b, (lambda k, col=col: wv[:, k, col * 128:(col + 1) * 128]), (lambda k: xgp(k))))
                    mm_multi(gl, range(KC), lambda k: [("XP", k)], [("slot", s)], KC, 0)
                else:
                    for name, col in (("C", 1), ("u", 2), ("v", 3), ("B", 0)):
                        b = next_bank()
                        u[name] = b
                        mm_group(b, lambda k, col=col: wv[:, k, col * 128:(col + 1) * 128], lambda k: xgp(k), KC,
                                 reads=xg_reads + [("slot", s)])
                if t == 0:
                    b = next_bank()
                    u["halo"] = b
                    for hi, col in enumerate((1, 2, 3)):
                        mm_group(b, lambda k, col=col: wv[:, k, col * 128:(col + 1) * 128], lambda k: xgh[:, k, :], KC,
                                 reads=["xgh", ("slot", s)], c_lo=16 * hi, ncols=HALO)

            def E(j):
                u = U[j]
                HQ = {"h": [], "q": []}
                cur_list = ["h"]

                def OP(eng, fn, reads, writes):
                    HQ[cur_list[0]].append((eng, fn, reads, writes))

                def ACOPY(out, in_, reads, writes):
                    OP("act", lambda e: e.activation(out=out, in_=in_, func=AF.Copy), reads, writes)
                hC, hp, hcu = K5(0), K5(1), K8(0)
                hy = K5(2 + j % 2)
                Csb, p, cu, y = W5(0), W5(1), W8(0), W5(2 + j % 2)
                u["y"] = 2 + j % 2
                bC, bu, bB, bv = u["C"], u["u"], u["B"], u["v"]
                ACOPY(Csb, banks[bC.i][:], [bC], hC)
                OP("dve", lambda e: e.tensor_tensor(out=cu[:, 2:2 + TT], in0=Csb, in1=banks[bu.i][:], op=ALU.mult),
                     reads=hC + [bu], writes=hcu)
                if t == 0:
                    bh = u["halo"]
                    ACOPY(small[:, 0:2], banks[bh.i][:, 14:16], [bh], ["small"])
                    OP("dve", lambda e: e.tensor_tensor(out=cu[:, 0:2], in0=small[:, 0:2], in1=banks[bh.i][:, 30:32],
                                                          op=ALU.mult),
                         reads=["small", bh], writes=hcu)
                else:
                    ACOPY(cu[:, 0:2], carry_cu[:, j, :], [("ccu", j)], hcu)
                if not last:
                    ACOPY(carry_cu[:, j, :], cu[:, TT:TT + 2], hcu, [("ccu", j)])
                OP("dve", lambda e: e.tensor_scalar(out=p, in0=cu[:, 2:2 + TT], scalar1=cst(C_CONV + 2 * 8 + j),
                                                      scalar2=None, op0=ALU.mult),
                     reads=hcu + ["consts"], writes=hp)
                OP("dve", lambda e: e.scalar_tensor_tensor(out=p, in0=cu[:, 1:1 + TT], scalar=cst(C_CONV + 1 * 8 + j),
                                                             in1=p, op0=ALU.mult, op1=ALU.add),
                     reads=hcu + hp + ["consts"], writes=hp)
                OP("dve", lambda e: e.scalar_tensor_tensor(out=p, in0=cu[:, 0:TT], scalar=cst(C_CONV + 0 * 8 + j),
                                                             in1=p, op0=ALU.mult, op1=ALU.add),
                     reads=hcu + hp + ["consts"], writes=hp)
                OP("dve", lambda e: e.tensor_tensor(out=y, in0=p, in1=banks[bB.i][:], op=ALU.mult),
                     reads=hp + [bB], writes=hy)
                q = next_sq()
                u["ysq"] = q
                OP("act", lambda e: e.activation(out=sqt[q.i][:], in_=y, func=AF.Square), hy, [q])
                cur_list[0] = "q"
                hv, hA, hB = K8(1), K8(2), K8(3)
                vb, sA, sB = W8(1), W8(2), W8(3)
                g = j // 2
                wwin = 2 << g
                ACOPY(vb[:, 16:16 + TT], banks[bv.i][:], [bv], hv)
                if t == 0:
                    ACOPY(vb[:, 0:16], banks[bh.i][:, 32:48], [bh], hv)
                else:
                    ACOPY(vb[:, 0:16], carry_v[:, j, :], [("cv", j)], hv)
                if not last:
                    ACOPY(carry_v[:, j, :], vb[:, TT:TT + 16], hv, [("cv", j)])
                OP("dve", lambda e: e.tensor_tensor(out=sA[:, 1:528], in0=vb[:, 1:528], in1=vb[:, 0:527], op=ALU.add),
                     reads=hv, writes=hA)
                cur, hcur = sA, hA
                if g >= 1:
                    OP("dve", lambda e: e.tensor_tensor(out=sB[:, 3:528], in0=sA[:, 3:528], in1=sA[:, 1:526], op=ALU.add),
                         reads=hA, writes=hB)
                    cur, hcur = sB, hB
                if g >= 2:
                    OP("dve", lambda e: e.tensor_tensor(out=sA[:, 7:528], in0=sB[:, 7:528], in1=sB[:, 3:524], op=ALU.add),
                         reads=hB, writes=hA)
                    cur, hcur = sA, hA
                if g >= 3:
                    OP("dve", lambda e: e.tensor_tensor(out=sB[:, 15:528], in0=sA[:, 15:528], in1=sA[:, 7:520], op=ALU.add),
                         reads=hA, writes=hB)
                    cur, hcur = sB, hB
                pj = j % 4
                OP("dve", lambda e: e.scalar_tensor_tensor(out=PL(pj), in0=cur[:, 16:16 + TT], scalar=1.0 / wwin,
                                                             in1=vb[:, 16:16 + TT], op0=ALU.mult, op1=ALU.subtract),
                     reads=hcur + hv, writes=KPL(pj))
                if t == 0:
                    OP("dve", lambda e: e.tensor_tensor(out=small[:, 16:32], in0=cur[:, 16:32],
                                                          in1=consts[:, C_INV + g * 16:C_INV + (g + 1) * 16], op=ALU.mult),
                         reads=hcur + ["consts"], writes=["small2"])
                    OP("dve", lambda e: e.tensor_tensor(out=PL(pj)[:, 0:16], in0=small[:, 16:32], in1=vb[:, 16:32],
                                                          op=ALU.subtract),
                         reads=["small2"] + hv, writes=KPL(pj))
                hl, ql = HQ["h"], HQ["q"]
                for n in range(max(len(hl), len(ql))):
                    if n < len(hl):
                        P.op(hl[n][0], hl[n][1], reads=hl[n][2], writes=hl[n][3])
                    if n < len(ql):
                        P.op(ql[n][0], ql[n][1], reads=ql[n][2], writes=ql[n][3])
                if t == 0 and j == 0:
                    dump(2, cu[:, 2:2 + TT], hcu)
                    dump(14, Csb, hC)
                    dump(3, y, hy)
                    dump(5, PL(pj), KPL(pj))

            def S(j):
                u = U[j]
                b = next_bank()
                u["hst"] = b
                stat_mm(b, u["ysq"], True, True)
                if j % 2 == 1:
                    g = j // 2
                    u["yp"] = []
                    for e2 in range(2):
                        b2 = next_bank()
                        u["yp"].append(b2)
                        mm_group(b2, lambda k, e2=e2: poolw[:, g, k, e2 * 128:(e2 + 1) * 128],
                                 lambda k: PL((2 * g + k) % 4), 2,
                                 reads=KPL((2 * g) % 4) + KPL((2 * g + 1) % 4) + ["poolw"])

            def F(j):
                u = U[j]
                r = rsqrt_from_bank(u["hst"], 128)
                iy = u["y"]
                P.op("dve", lambda e: e.tensor_tensor(out=mixed(j), in0=W5(iy), in1=sct[r.i][:], op=ALU.mult),
                     reads=K5(iy) + [r], writes=[("A", 16 + j)])
                if t == 0 and j == 0:
                    dump(4, mixed(0), [("A", 16)])
                if j % 2 == 1:
                    u["yps"] = []
                    u["ypsq"] = []
                    for e2 in range(2):
                        iw = 4 + e2
                        q = next_sq()
                        b2 = u["yp"][e2]
                        u["yps"].append(iw)
                        u["ypsq"].append(q)
                        act_copy(W5(iw), banks[b2.i][:], [b2], K5(iw))
                        act_square(sqt[q.i][:], banks[b2.i][:], [b2], [q])

            def S2(j):
                if j % 2 != 1:
                    return
                u = U[j]
                b = next_bank()
                u["pst"] = b
                for e2 in range(2):
                    stat_mm(b, u["ypsq"][e2], e2 == 0, e2 == 1)

            def F2(j):
                if j % 2 != 1:
                    return
                u = U[j]
                g = j // 2
                r = rsqrt_from_bank(u["pst"], 256)
                for e2 in range(2):
                    iw = u["yps"][e2]
                    c = 2 * g + e2
                    P.op("dve", lambda e, iw=iw, c=c: e.scalar_tensor_tensor(out=mixed(8 + c), in0=W5(iw),
                                                                             scalar=cst(C_PSC + c), in1=sct[r.i][:],
                                                                             op0=ALU.mult, op1=ALU.mult),
                         reads=K5(iw) + [r, "consts"], writes=[("A", 16 + 8 + c)])
                if t == 0 and j == 1:
                    dump(6, mixed(8), [("A", 24)])

            OUT_EARLY = [0, 1, 2, 3, 4, 5, 6, 8, 9, 10, 11, 12, 13]
            OUT_LATE = [7, 14, 15]
            op0 = {}

            def out_stage0_early():
                s0, wv0 = use_stage("out")
                op0["s"], op0["wv"] = s0, wv0
                op0["banks"] = [next_bank(hold=True) for _ in range(4)]
                op0["gl"] = [(op0["banks"][i], (lambda k, i=i: wv0[:, k, i * 128:(i + 1) * 128]), (lambda k: mixed(k)))
                             for i in range(4)]
                mm_multi(op0["gl"], OUT_EARLY, lambda k: [("A", 16 + k)], [("slot", s0)], KC, 0)

            for step in range(8 + 2):
                if step < 8:
                    M(step)
                if step == 8:
                    out_stage0_early()
                if 0 <= step - 1 < 8:
                    S(step - 1)
                if 0 <= step - 2 < 8:
                    S2(step - 2)
                if step < 8:
                    E(step)
                if step < 8 and prev_tail:
                    for st_ in prev_tail[2 * step:2 * step + 2]:
                        st_()
                    if step == 7:
                        load_xres()
                if 0 <= step - 1 < 8:
                    F(step - 1)
                if 0 <= step - 2 < 8:
                    F2(step - 2)

            mixed_reads = [("A", 16 + k) for k in range(KC)]
            sb1 = next_bank(hold=True)
            pend = None

            def evac_mo(i, b):
                q = next_sq()
                act_square(sqt[q.i][:], banks[b.i][:], [b], [q])
                stat_mm(sb1, q, i == 0, i == KC - 1)
                act_copy(mov(i), banks[b.i][:], [b], [("B", 2 * i), ("B", 2 * i + 1)])

            mm_multi(op0["gl"], OUT_LATE, lambda k: [("A", 16 + k)], [("slot", op0["s"])], KC, len(OUT_EARLY))
            for b in op0["banks"]:
                release(b)
            for i in range(3):
                evac_mo(i, op0["banks"][i])
            pend = (3, op0["banks"][3])
            for i in range(4, KC):
                if i % 4 == 0:
                    s, wv = use_stage("out")
                b = next_bank()
                mm_group(b, lambda k, i=i, wv=wv: wv[:, k, (i % 4) * 128:(i % 4 + 1) * 128], lambda k: mixed(k), KC,
                         reads=mixed_reads + [("slot", s)])
                if pend is not None:
                    evac_mo(*pend)
                pend = (i, b)
            evac_mo(*pend)
            rs1 = rsqrt_from_bank(sb1, D)
            release(sb1)
            sb2 = next_bank(hold=True)
            s_g0, wv_g0 = use_stage("gu")
            gb = [next_bank() for _ in range(4)]
            gcols = [0, 256, 128, 384]
            gl0 = [(gb[n], (lambda k, c=gcols[n]: wv_g0[:, k, c:c + 128]), (lambda k: xg(k))) for n in range(4)]
            BURST_AFTER = {2: range(0, 3), 5: range(3, 6), 8: range(6, 9), 11: range(9, 12), 13: range(12, 14), 15: range(14, 16)}
            def x1_stt(i):
                P.op("dve", lambda e, i=i: e.scalar_tensor_tensor(out=mov(i), in0=mov(i), scalar=cst(C_G1 + i),
                                                                  in1=sct[rs1.i][:], op0=ALU.mult, op1=ALU.mult),
                     reads=[("B", 2 * i), ("B", 2 * i + 1), rs1, "consts"], writes=[("B", 2 * i), ("B", 2 * i + 1)])

            x1_stt(0)
            for i in range(KC):
                if i + 1 < KC:
                    x1_stt(i + 1)
                P.op("dve", lambda e, i=i: e.tensor_tensor(out=xres[:, i, :], in0=mov(i), in1=xres[:, i, :], op=ALU.add),
                     reads=[("B", 2 * i), ("B", 2 * i + 1), ("x", i)], writes=[("x", i)])
                q = next_sq()
                act_square(sqt[q.i][:], xres[:, i, :], [("x", i)], [q])
                stat_mm(sb2, q, i == 0, i == KC - 1)
                P.op("act", lambda e, i=i: e.activation(out=xg(i), in_=xres[:, i, :], func=AF.Copy, scale=cst(C_G2 + i)),
                     reads=[("x", i), "consts"], writes=[("A", i)])
                if i in BURST_AFTER:
                    ks = list(BURST_AFTER[i])
                    mm_multi(gl0, ks, lambda k: [("A", k)], [("slot", s_g0)], KC, ks[0])
            r2 = rsqrt_from_bank(sb2, D)
            release(sb2)
            if t == 0:
                dump(7, xres[:, 0, :], [("x", 0)])
                dump(11, sct[rs1.i][:], [rs1])
                dump(12, sct[r2.i][:], [r2])

            hf_reads = [("A", k) for k in range(KC)]
            pend = None
            sgi = [0]

            pf_steps = make_prefetch(t + 1) if not last else []
            pf_pos = [0]

            def pf_advance(n):
                for _ in range(n):
                    if pf_pos[0] < len(pf_steps):
                        pf_steps[pf_pos[0]]()
                        pf_pos[0] += 1

            def evac_gu(f, bg, bu):
                hg = P.alloc("w3", 2 * (sgi[0] % 2))
                hu = P.alloc("w3", 2 * (sgi[0] % 2) + 1)
                sgi[0] += 1
                tg, tu = w3[hg.i], w3[hu.i]
                P.op("dve", lambda e: e.tensor_tensor(out=tg[:], in0=banks[bg.i][:], in1=sct[r2.i][:], op=ALU.mult),
                     reads=[bg, r2], writes=[hg])
                P.op("dve", lambda e: e.tensor_tensor(out=tu[:], in0=banks[bu.i][:], in1=sct[r2.i][:], op=ALU.mult),
                     reads=[bu, r2], writes=[hu])
                P.op("act", lambda e: e.activation(out=tg[:], in_=tg[:], func=AF.Silu),
                     reads=[hg], writes=[hg])
                P.op("dve", lambda e: e.tensor_tensor(out=regB[:, f, :], in0=tg[:], in1=tu[:], op=ALU.mult),
                     reads=[hg, hu], writes=[("B", f)])

            evac_gu(0, gb[0], gb[1])
            pend = (1, gb[2], gb[3])
            pf_advance(4)
            for f in range(2, FC):
                if f % 2 == 0:
                    s, wv = use_stage("gu")
                e2 = f % 2
                bg = next_bank()
                mm_group(bg, lambda k, wv=wv, e2=e2: wv[:, k, e2 * 128:(e2 + 1) * 128], lambda k: xg(k), KC,
                         reads=hf_reads + [("slot", s)])
                bu = next_bank()
                mm_group(bu, lambda k, wv=wv, e2=e2: wv[:, k, 256 + e2 * 128:256 + (e2 + 1) * 128], lambda k: xg(k), KC,
                         reads=hf_reads + [("slot", s)])
                if pend is not None:
                    evac_gu(*pend)
                pend = (f, bg, bu)
                pf_advance(2)
            evac_gu(*pend)
            pf_advance(len(pf_steps))
            if t == 0:
                dump(9, regB[:, 0, :], [("B", 0)])

            sb3 = next_bank(hold=True)
            pend = []

            def evac_ff(i, b):
                q = next_sq()
                act_square(sqt[q.i][:], banks[b.i][:], [b], [q])
                stat_mm(sb3, q, i == 0, i == KC - 1)
                act_copy(ffv(i), banks[b.i][:], [b], [("A", 2 * i), ("A", 2 * i + 1)])

            for c2 in range(8):
                bb = [next_bank(), next_bank()]
                for h in range(2):
                    s, wv = use_stage("dn")
                    for e2 in range(2):
                        mm_group(bb[e2], lambda k, wv=wv, e2=e2: wv[:, k, e2 * 128:(e2 + 1) * 128],
                                 lambda k, h=h: regB[:, h * 22 + k, :], 22,
                                 reads=[("B", h * 22 + k) for k in range(22)] + [("slot", s)],
                                 start=(h == 0), stop=(h == 1))
                for (i, b) in pend:
                    evac_ff(i, b)
                pend = [(2 * c2, bb[0]), (2 * c2 + 1, bb[1])]
            for (i, b) in pend:
                evac_ff(i, b)
            rsqrt_from_bank(sb3, D, fixed=sc_rs3)
            release(sb3)
            if t == 0:
                dump(10, ffv(0), [("A", 0), ("A", 1)])
                dump(13, sc_rs3[:], ["rs3"])

            def tail_scale(i):
                P.op("dve", lambda e: e.scalar_tensor_tensor(out=ffv(i), in0=ffv(i), scalar=cst(C_G3 + i),
                                                             in1=sc_rs3[:], op0=ALU.mult, op1=ALU.mult),
                     reads=[("A", 2 * i), ("A", 2 * i + 1), "rs3", "consts"], writes=[("A", 2 * i), ("A", 2 * i + 1)])

            def tail_add(i, nstore=4):
                P.op("dve", lambda e: e.tensor_tensor(out=xres[:, i, :], in0=ffv(i), in1=xres[:, i, :], op=ALU.add),
                     reads=[("A", 2 * i), ("A", 2 * i + 1), ("x", i)], writes=[("x", i)])
                if i % nstore == nstore - 1:
                    lo = i - nstore + 1
                    q = i // 4
                    P.dma("sp", lambda e: e.dma_start(out=yT_v[:, lo:i + 1, t * TT:(t + 1) * TT],
                                                      in_=xres[:, lo:i + 1, :]).then_inc(sems[f"o{q}"], 16),
                          f"o{q}", reads=[("x", k) for k in range(lo, i + 1)])

            if last:
                tail_scale(0)
                for i in range(KC):
                    if i + 1 < KC:
                        tail_scale(i + 1)
                    tail_add(i, nstore=2)
                return []

            order = [8, 12, 9, 13, 10, 14, 11, 15, 0, 1, 2, 3, 4, 5, 6, 7]

            def pair_step(a, b):
                def f():
                    tail_scale(a)
                    tail_scale(b)
                    tail_add(a)
                    tail_add(b)
                return f
            steps = []
            for n in range(0, KC, 2):
                steps.append(pair_step(order[n], order[n + 1]))
                steps.append(lambda: None)
            return steps

        tail = []
        for t in range(NTILE):
            tail = emit_tile(t, tail)
        P.wait_all("sp", [(f"o{q}", P.dma_cnt[f"o{q}"]) for q in range(4)])

        @block.sync
        def _(e):
            P.replay("sp", e, sems)

        @block.gpsimd
        def _(e):
            P.replay("pool", e, sems)

        @block.tensor
        def _(e):
            P.replay("pe", e, sems)

        @block.scalar
        def _(e):
            P.replay("act", e, sems)

        @block.vector
        def _(e):
            P.replay("dve", e, sems)

    return nc


_PROGRAM = None


def _host_layout(x, ln_mix_pre, w_in, conv_w, pool_w, pool_scale, w_out, ln_mix_post,
                 ln_ffn_pre, w_gate, w_up, w_down, ln_ffn_post):
    f32 = np.float32
    x = np.asarray(x, f32)
    w_in = np.asarray(w_in, f32)[0]
    w_in_p = w_in.reshape(D, 4, 8, 128).transpose(0, 2, 1, 3).reshape(D, 4096)
    wg = np.asarray(w_gate, f32)[0].reshape(D, 22, 256)
    wu = np.asarray(w_up, f32)[0].reshape(D, 22, 256)
    w_gu = np.stack([wg, wu], axis=2).reshape(D, 2 * DFF)

    def stage_major(w, ncol):
        K, N = w.shape
        return np.ascontiguousarray(
            w.reshape(K // 128, 128, N // ncol, ncol).transpose(2, 1, 0, 3).reshape(N // ncol, 128, (K // 128) * ncol))

    wst_in = stage_major(w_in_p, 512)
    wst_out = stage_major(np.asarray(w_out, f32)[0], 512)
    wst_gu = stage_major(w_gu, 512)
    wd = np.asarray(w_down, f32)[0]
    wst_dn = np.ascontiguousarray(
        wd.reshape(2, 22, 128, 8, 256).transpose(3, 0, 2, 1, 4).reshape(16, 128, 22 * 256))
    pool_w_ = np.ascontiguousarray(np.asarray(pool_w, f32)[0])

    def pk(v):
        return np.asarray(v, f32).reshape(-1, 128).T

    base = np.zeros((128, NCONST), f32)
    base[:, C_G0:C_G0 + 16] = pk(ln_mix_pre[0])
    base[:, C_G1:C_G1 + 16] = pk(ln_mix_post[0])
    base[:, C_G2:C_G2 + 16] = pk(ln_ffn_pre[0])
    base[:, C_G3:C_G3 + 16] = pk(ln_ffn_post[0])
    cw = np.asarray(conv_w, f32)[0]
    for j in range(3):
        base[:, C_CONV + j * 8:C_CONV + (j + 1) * 8] = cw[j].reshape(8, 128).T
    base[:, C_PSC:C_PSC + 8] = np.asarray(pool_scale, f32)[0].reshape(8, 128).T

    in_maps = []
    for c in range(N_CORES):
        b, q = divmod(c, 4)
        xh = np.zeros((D, HALO + TOK), f32)
        lo = q * TOK - HALO
        if lo >= 0:
            xh[:, :] = x[b, lo:(q + 1) * TOK, :].T
        else:
            xh[:, HALO:] = x[b, 0:TOK, :].T
        cst = base.copy()
        for g, w in enumerate((2, 4, 8, 16)):
            for i in range(16):
                cnt = min(i + 1, w) if q == 0 else w
                cst[:, C_INV + g * 16 + i] = 1.0 / cnt
        in_maps.append({"xh": xh, "consts": cst, "wst_in": wst_in, "pool_w": pool_w_, "wst_out": wst_out,
                        "wst_gu": wst_gu, "wst_dn": wst_dn})
    return in_maps


def kernel(x, ln_mix_pre, w_in, conv_w, pool_w, pool_scale, w_out, ln_mix_post,
           ln_ffn_pre, w_gate, w_up, w_down, ln_ffn_post):
    global _PROGRAM
    in_maps = _host_layout(x, ln_mix_pre, w_in, conv_w, pool_w, pool_scale, w_out, ln_mix_post,
                           ln_ffn_pre, w_gate, w_up, w_down, ln_ffn_post)
    if _PROGRAM is None:
        _PROGRAM = build_program()
    res = run_bass_kernel_spmd(_PROGRAM, in_maps, core_ids=list(range(N_CORES)))
    out = np.empty((2, 4 * TOK, D), np.float32)
    for c in range(N_CORES):
        b, q = divmod(c, 4)
        out[b, q * TOK:(q + 1) * TOK, :] = np.asarray(res.results[c]["yT"], np.float32).T
    return out
```
